# Optimizing a Trainium2 kernel written in Bass

```python
import jax
import jax.numpy as jnp
from jax import lax
import numpy as np

D_MODEL = 1024
BATCH = 4
SEQ = 4096
DEPTH = 2
DEC_BATCH = 16
DEC_SEQ = 32
PAST_LEN = 1024

CHUNK = 64
N_GROUPS = 4
GROUP_WIDTH = D_MODEL // N_GROUPS
MIX_WIDTH = N_GROUPS * GROUP_WIDTH
CONV_WIDTH = 31
CONV_HIST = CONV_WIDTH - 1
RWKV_HEAD = 64
RWKV_HEADS = GROUP_WIDTH // RWKV_HEAD
DECAY_RANK = 32
AAA_RANK = 32
GATE_RANK = 64
POOL_WINDOWS = (2, 4, 8, 16)
POOL_GROUPS = len(POOL_WINDOWS)
POOL_CH = GROUP_WIDTH // POOL_GROUPS
POOL_HIST = max(POOL_WINDOWS) - 1
HEAD_DIM = 64
N_Q_HEADS = GROUP_WIDTH // HEAD_DIM
N_KV_HEADS = 2
GQA_GROUP = N_Q_HEADS // N_KV_HEADS
WINDOW = 128
WIN_CHUNKS = WINDOW // CHUNK
D_FF = -(-(8 * D_MODEL) // (3 * 256)) * 256

A_COLS = 2 * GROUP_WIDTH
B_COLS = 3 * GROUP_WIDTH + DECAY_RANK + AAA_RANK + GATE_RANK
C_COLS = GROUP_WIDTH
D_COLS = (N_Q_HEADS + 2 * N_KV_HEADS) * HEAD_DIM
IN_COLS = A_COLS + B_COLS + C_COLS + D_COLS
IN_SPLITS = (A_COLS, A_COLS + B_COLS, A_COLS + B_COLS + C_COLS)
B_SPLITS = (GROUP_WIDTH, 2 * GROUP_WIDTH, 3 * GROUP_WIDTH,
            3 * GROUP_WIDTH + DECAY_RANK, 3 * GROUP_WIDTH + DECAY_RANK + AAA_RANK)
D_SPLITS = (N_Q_HEADS * HEAD_DIM, (N_Q_HEADS + N_KV_HEADS) * HEAD_DIM)
RMS_EPS = 1e-6
LN_EPS = 1e-5
GN_EPS = 64e-5
ATTN_SCALE = HEAD_DIM ** -0.5
NEG_INF = -1e30

kernel_name = 'hybrid_streaming_encoder_step'


def rms_norm(x, g):
    x32 = x.astype(jnp.float32)
    y = x32 * lax.rsqrt(jnp.mean(x32 * x32, axis=-1, keepdims=True) + RMS_EPS)
    return (y * g.astype(jnp.float32)).astype(x.dtype)


def layer_norm(x, g, b):
    x32 = x.astype(jnp.float32)
    mu = jnp.mean(x32, axis=-1, keepdims=True)
    var = jnp.mean(jnp.square(x32 - mu), axis=-1, keepdims=True)
    y = (x32 - mu) * lax.rsqrt(var + LN_EPS) * g.astype(jnp.float32) + b.astype(jnp.float32)
    return y.astype(x.dtype)


def conv_module(u, hist, conv_w, conv_b, ln_g, ln_b):
    val, gate = jnp.split(u, 2, axis=-1)
    glu = val * jax.nn.sigmoid(gate)
    ext = jnp.concatenate([hist.astype(glu.dtype), glu], axis=1)
    y = lax.conv_general_dilated(
        ext, conv_w[:, None, :].astype(ext.dtype), window_strides=(1,), padding='VALID',
        dimension_numbers=('NWC', 'WIO', 'NWC'), feature_group_count=GROUP_WIDTH)
    y = jax.nn.silu(layer_norm(y + conv_b, ln_g, ln_b))
    return y, ext[:, -CONV_HIST:]


def wkv7_step(S, inp):
    r_t, w_t, k_t, v_t, a_t, b_t = inp
    sa = jnp.einsum('bhvk,bhk->bhv', S, a_t)
    S = S * w_t[:, :, None, :] + sa[..., None] * b_t[:, :, None, :] + v_t[..., None] * k_t[:, :, None, :]
    return S, jnp.einsum('bhvk,bhk->bhv', S, r_t)


def rwkv7_mixer(u, shift_prev, wkv_state, mu, w0, w2, a0, a2, g2, k_k, k_a, r_k, gn_g, gn_b):
    bsz, seq_len, _ = u.shape
    f32 = jnp.float32
    prev = jnp.concatenate([shift_prev[:, None, :].astype(u.dtype), u[:, :-1]], axis=1)
    xs = u + mu * (prev - u)
    r, k, v, lat_w, lat_a, lat_g = jnp.split(xs, B_SPLITS, axis=-1)
    w_log = -jax.nn.softplus(-(w0 + jnp.tanh(lat_w) @ w2).astype(f32)) - 0.5
    decay = jnp.exp(-jnp.exp(w_log))
    a_rate = jax.nn.sigmoid((a0 + lat_a @ a2).astype(f32))
    gate = jax.nn.sigmoid(lat_g) @ g2

    def heads(t):
        return t.astype(f32).reshape(bsz, seq_len, RWKV_HEADS, RWKV_HEAD)

    kk = heads(k * k_k)
    kk = kk / jnp.maximum(jnp.sqrt(jnp.sum(kk * kk, axis=-1, keepdims=True)), 1e-12)
    k_mod = heads(k.astype(f32) * (1.0 + (a_rate - 1.0) * k_a.astype(f32)))
    rh, vh, ah, wh = heads(r), heads(v), heads(a_rate), heads(decay)
    scan_in = tuple(jnp.moveaxis(t, 1, 0) for t in (rh, wh, k_mod, vh, -kk, kk * ah))
    s_final, y = lax.scan(wkv7_step, wkv_state.astype(f32), scan_in)
    y = jnp.moveaxis(y, 0, 1)
    m = jnp.mean(y, axis=-1, keepdims=True)
    var = jnp.mean(jnp.square(y - m), axis=-1, keepdims=True)
    y = ((y - m) * lax.rsqrt(var + GN_EPS)).reshape(bsz, seq_len, GROUP_WIDTH)
    y = y * gn_g.astype(f32) + gn_b.astype(f32)
    bonus = jnp.sum(rh * k_mod * r_k.astype(f32), axis=-1, keepdims=True) * vh
    y = (y + bonus.reshape(bsz, seq_len, GROUP_WIDTH)) * gate.astype(f32)
    return y.astype(u.dtype), s_final.astype(u.dtype), u[:, -1]


def pool_mixer(u, hist, pos0, pool_w, pool_scale):
    bsz, seq_len, _ = u.shape
    ext = jnp.concatenate([hist.astype(u.dtype), u], axis=1)
    cs = jnp.pad(jnp.cumsum(ext.astype(jnp.float32), axis=1), ((0, 0), (1, 0), (0, 0)))
    pos = pos0 + jnp.arange(seq_len)
    means = []
    for gi, w in enumerate(POOL_WINDOWS):
        sl = slice(gi * POOL_CH, (gi + 1) * POOL_CH)
        end = cs[:, POOL_HIST + 1:POOL_HIST + 1 + seq_len, sl]
        start = cs[:, POOL_HIST + 1 - w:POOL_HIST + 1 - w + seq_len, sl]
        cnt = jnp.minimum(w, pos + 1).astype(jnp.float32)[None, :, None]
        means.append((end - start) / cnt)
    d = (jnp.concatenate(means, axis=-1) - u.astype(jnp.float32)).reshape(bsz, seq_len, POOL_GROUPS, POOL_CH)
    y = jnp.einsum('blgc,gcd->blgd', d, pool_w.astype(jnp.float32)).reshape(bsz, seq_len, GROUP_WIDTH)
    return (y * pool_scale.astype(jnp.float32)).astype(u.dtype), ext[:, -POOL_HIST:]


def sink_softmax(s, sinks):
    sk = sinks.astype(jnp.float32).reshape(N_KV_HEADS, GQA_GROUP)[:, :, None, None]
    m = jnp.maximum(jnp.max(s, axis=-1, keepdims=True), sk)
    p = jnp.exp(s - m)
    return p / (jnp.sum(p, axis=-1, keepdims=True) + jnp.exp(sk - m))


def attention_qkv(u, q_norm_g, k_norm_g):
    bsz, seq_len, _ = u.shape
    q, k, v = jnp.split(u, D_SPLITS, axis=-1)
    q = rms_norm(q.reshape(bsz, seq_len, N_Q_HEADS, HEAD_DIM), q_norm_g)
    k = rms_norm(k.reshape(bsz, seq_len, N_KV_HEADS, HEAD_DIM), k_norm_g)
    v = v.reshape(bsz, seq_len, N_KV_HEADS, HEAD_DIM)
    return q, k, v


def window_attention_prompt(q, k, v, sinks):
    bsz, seq_len = q.shape[0], q.shape[1]
    n_chunks = seq_len // CHUNK
    qc = q.reshape(bsz, n_chunks, CHUNK, N_KV_HEADS, GQA_GROUP, HEAD_DIM)

    def band(t):
        tc = t.reshape(bsz, n_chunks, CHUNK, N_KV_HEADS, HEAD_DIM)
        tp = jnp.pad(tc, ((0, 0), (WIN_CHUNKS, 0), (0, 0), (0, 0), (0, 0)))
        return jnp.concatenate([tp[:, j:j + n_chunks] for j in range(WIN_CHUNKS + 1)], axis=2)

    kb, vb = band(k), band(v)
    key_chunk = jnp.arange(n_chunks)[:, None] + jnp.arange(WIN_CHUNKS + 1)[None, :] - WIN_CHUNKS
    valid = jnp.repeat(key_chunk >= 0, CHUNK, axis=1)
    s = jnp.einsum('bnqhgd,bnkhd->bnhgqk', qc, kb).astype(jnp.float32) * ATTN_SCALE
    s = jnp.where(valid[None, :, None, None, None, :], s, NEG_INF)
    p = sink_softmax(s, sinks).astype(v.dtype)
    o = jnp.einsum('bnhgqk,bnkhd->bnqhgd', p, vb)
    return o.reshape(bsz, seq_len, N_Q_HEADS * HEAD_DIM)


def window_attention_sample(q, k, v, k_cache, v_cache, sinks):
    bsz, t = q.shape[0], q.shape[1]
    kf = jnp.concatenate([k_cache.astype(k.dtype), k], axis=1)
    vf = jnp.concatenate([v_cache.astype(v.dtype), v], axis=1)
    qg = q.reshape(bsz, t, N_KV_HEADS, GQA_GROUP, HEAD_DIM)
    s = jnp.einsum('bqhgd,bkhd->bhgqk', qg, kf).astype(jnp.float32) * ATTN_SCALE
    p = sink_softmax(s, sinks).astype(v.dtype)
    o = jnp.einsum('bhgqk,bkhd->bqhgd', p, vf).reshape(bsz, t, N_Q_HEADS * HEAD_DIM)
    return o, kf[:, -WINDOW:], vf[:, -WINDOW:]


def run_trunk(x, conv_hist, rwkv_state, shift_prev, pool_hist, k_cache, v_cache, pos0, w):
    bsz = x.shape[0]
    new_conv, new_rwkv, new_shift, new_pool, new_k, new_v = [], [], [], [], [], []
    for l in range(DEPTH):
        if conv_hist is None:
            ch = jnp.zeros((bsz, CONV_HIST, GROUP_WIDTH), x.dtype)
            rs = jnp.zeros((bsz, RWKV_HEADS, RWKV_HEAD, RWKV_HEAD), x.dtype)
            sp = jnp.zeros((bsz, B_COLS), x.dtype)
            ph = jnp.zeros((bsz, POOL_HIST, GROUP_WIDTH), x.dtype)
        else:
            ch, rs, sp, ph = conv_hist[l], rwkv_state[l], shift_prev[l], pool_hist[l]
        xn = rms_norm(x, w['norm_mix_g'][l])
        proj = xn @ w['w_in'][l]
        u_a, u_b, u_c, u_d = jnp.split(proj, IN_SPLITS, axis=-1)
        y_a, ch_new = conv_module(u_a, ch, w['conv_w'][l], w['conv_b'][l], w['conv_ln_g'][l], w['conv_ln_b'][l])
        y_b, rs_new, sp_new = rwkv7_mixer(
            u_b, sp, rs, w['rwkv_mu'][l], w['rwkv_w0'][l], w['rwkv_w2'][l], w['rwkv_a0'][l], w['rwkv_a2'][l],
            w['rwkv_g2'][l], w['rwkv_k_k'][l], w['rwkv_k_a'][l], w['rwkv_r_k'][l], w['rwkv_gn_g'][l], w['rwkv_gn_b'][l])
        y_c, ph_new = pool_mixer(u_c, ph, pos0, w['pool_w'][l], w['pool_scale'][l])
        q, k, v = attention_qkv(u_d, w['attn_q_norm'][l], w['attn_k_norm'][l])
        if k_cache is None:
            y_d = window_attention_prompt(q, k, v, w['attn_sinks'][l])
            k_new, v_new = k[:, -WINDOW:], v[:, -WINDOW:]
        else:
            y_d, k_new, v_new = window_attention_sample(q, k, v, k_cache[l], v_cache[l], w['attn_sinks'][l])
        x = x + jnp.concatenate([y_a, y_b, y_c, y_d], axis=-1) @ w['w_out'][l]
        hn = rms_norm(x, w['norm_ffn_g'][l])
        x = x + (jax.nn.silu(hn @ w['ffn_w_gate'][l]) * (hn @ w['ffn_w_up'][l])) @ w['ffn_w_down'][l]
        new_conv.append(ch_new)
        new_rwkv.append(rs_new)
        new_shift.append(sp_new)
        new_pool.append(ph_new)
        new_k.append(k_new)
        new_v.append(v_new)
    return x, (jnp.stack(new_conv), jnp.stack(new_rwkv), jnp.stack(new_shift),
               jnp.stack(new_pool), jnp.stack(new_k), jnp.stack(new_v))


def setup_inputs(seed: int = 0) -> dict:
    key = jax.random.key(seed)
    ks = jax.random.split(key, 40)
    f32 = jnp.float32

    def nrm(i, shape, scale=1.0, shift=0.0):
        return shift + scale * jax.random.normal(ks[i], shape, f32)

    return {
        'x_prompt': nrm(0, (BATCH, SEQ, D_MODEL)),
        'x_sample': nrm(1, (DEC_BATCH, DEC_SEQ, D_MODEL)),
        'cache_conv': nrm(2, (DEPTH, DEC_BATCH, CONV_HIST, GROUP_WIDTH), 0.5),
        'state_rwkv': nrm(3, (DEPTH, DEC_BATCH, RWKV_HEADS, RWKV_HEAD, RWKV_HEAD), 0.5),
        'state_rwkv_shift': nrm(4, (DEPTH, DEC_BATCH, B_COLS)),
        'cache_pool': nrm(5, (DEPTH, DEC_BATCH, POOL_HIST, GROUP_WIDTH)),
        'cache_k': nrm(6, (DEPTH, DEC_BATCH, WINDOW, N_KV_HEADS, HEAD_DIM)),
        'cache_v': nrm(7, (DEPTH, DEC_BATCH, WINDOW, N_KV_HEADS, HEAD_DIM)),
        'norm_mix_g': nrm(8, (DEPTH, D_MODEL), 0.1, 1.0),
        'w_in': nrm(9, (DEPTH, D_MODEL, IN_COLS), D_MODEL ** -0.5),
        'conv_w': nrm(10, (DEPTH, CONV_WIDTH, GROUP_WIDTH), CONV_WIDTH ** -0.5),
        'conv_b': nrm(11, (DEPTH, GROUP_WIDTH), 0.02),
        'conv_ln_g': nrm(12, (DEPTH, GROUP_WIDTH), 0.1, 1.0),
        'conv_ln_b': nrm(13, (DEPTH, GROUP_WIDTH), 0.02),
        'rwkv_mu': jax.random.uniform(ks[14], (DEPTH, B_COLS), f32),
        'rwkv_w0': nrm(15, (DEPTH, GROUP_WIDTH), 0.5, -1.0),
        'rwkv_w2': nrm(16, (DEPTH, DECAY_RANK, GROUP_WIDTH), 0.5 * DECAY_RANK ** -0.5),
        'rwkv_a0': nrm(17, (DEPTH, GROUP_WIDTH), 0.5),
        'rwkv_a2': nrm(18, (DEPTH, AAA_RANK, GROUP_WIDTH), 0.5 * AAA_RANK ** -0.5),
        'rwkv_g2': nrm(19, (DEPTH, GATE_RANK, GROUP_WIDTH), GATE_RANK ** -0.5),
        'rwkv_k_k': nrm(20, (DEPTH, GROUP_WIDTH), 0.1, 1.0),
        'rwkv_k_a': nrm(21, (DEPTH, GROUP_WIDTH), 0.1, 1.0),
        'rwkv_r_k': nrm(22, (DEPTH, RWKV_HEADS, RWKV_HEAD), 0.1),
        'rwkv_gn_g': nrm(23, (DEPTH, GROUP_WIDTH), 0.1, 1.0),
        'rwkv_gn_b': nrm(24, (DEPTH, GROUP_WIDTH), 0.02),
        'pool_w': nrm(25, (DEPTH, POOL_GROUPS, POOL_CH, POOL_CH), POOL_CH ** -0.5),
        'pool_scale': nrm(26, (DEPTH, GROUP_WIDTH), 0.1, 1.0),
        'attn_q_norm': nrm(27, (DEPTH, HEAD_DIM), 0.1, 1.0),
        'attn_k_norm': nrm(28, (DEPTH, HEAD_DIM), 0.1, 1.0),
        'attn_sinks': nrm(29, (DEPTH, N_Q_HEADS), 0.5),
        'w_out': nrm(30, (DEPTH, MIX_WIDTH, D_MODEL), MIX_WIDTH ** -0.5),
        'norm_ffn_g': nrm(31, (DEPTH, D_MODEL), 0.1, 1.0),
        'ffn_w_gate': nrm(32, (DEPTH, D_MODEL, D_FF), D_MODEL ** -0.5),
        'ffn_w_up': nrm(33, (DEPTH, D_MODEL, D_FF), D_MODEL ** -0.5),
        'ffn_w_down': nrm(34, (DEPTH, D_FF, D_MODEL), D_FF ** -0.5),
    }


def reference(x_prompt, x_sample, cache_conv, state_rwkv, state_rwkv_shift, cache_pool, cache_k, cache_v,
              norm_mix_g, w_in, conv_w, conv_b, conv_ln_g, conv_ln_b, rwkv_mu, rwkv_w0, rwkv_w2, rwkv_a0,
              rwkv_a2, rwkv_g2, rwkv_k_k, rwkv_k_a, rwkv_r_k, rwkv_gn_g, rwkv_gn_b, pool_w, pool_scale,
              attn_q_norm, attn_k_norm, attn_sinks, w_out, norm_ffn_g, ffn_w_gate, ffn_w_up, ffn_w_down):
    w = {
        'norm_mix_g': norm_mix_g, 'w_in': w_in, 'conv_w': conv_w, 'conv_b': conv_b,
        'conv_ln_g': conv_ln_g, 'conv_ln_b': conv_ln_b, 'rwkv_mu': rwkv_mu, 'rwkv_w0': rwkv_w0,
        'rwkv_w2': rwkv_w2, 'rwkv_a0': rwkv_a0, 'rwkv_a2': rwkv_a2, 'rwkv_g2': rwkv_g2,
        'rwkv_k_k': rwkv_k_k, 'rwkv_k_a': rwkv_k_a, 'rwkv_r_k': rwkv_r_k, 'rwkv_gn_g': rwkv_gn_g,
        'rwkv_gn_b': rwkv_gn_b, 'pool_w': pool_w, 'pool_scale': pool_scale, 'attn_q_norm': attn_q_norm,
        'attn_k_norm': attn_k_norm, 'attn_sinks': attn_sinks, 'w_out': w_out, 'norm_ffn_g': norm_ffn_g,
        'ffn_w_gate': ffn_w_gate, 'ffn_w_up': ffn_w_up, 'ffn_w_down': ffn_w_down,
    }
    y_prompt, (conv_p, rwkv_p, shift_p, pool_p, k_p, v_p) = run_trunk(
        x_prompt, None, None, None, None, None, None, 0, w)
    y_sample, (conv_s, rwkv_s, shift_s, pool_s, k_s, v_s) = run_trunk(
        x_sample, cache_conv, state_rwkv, state_rwkv_shift, cache_pool, cache_k, cache_v, PAST_LEN, w)
    return (y_prompt, y_sample, conv_p, conv_s, rwkv_p, rwkv_s, shift_p, shift_s,
            pool_p, pool_s, k_p, k_s, v_p, v_s)
```

```python
import contextlib
import os
DBG = os.environ.get('KDBG', 'ACDBOFR')
KB = int(os.environ.get('KB', '9'))
STRICT_ENGS = os.environ.get('KSE', '').split(',')
STRICT = os.environ.get('KSTRICT', '0') == '1'
import numpy as np
import concourse.bass as bass
import concourse.mybir as mybir
from concourse.bass_utils import run_bass_kernel_spmd

F32 = mybir.dt.float32
BF16 = mybir.dt.bfloat16
CH = BF16
ALU = mybir.AluOpType
AF = mybir.ActivationFunctionType

TB = 128
NPV = 111
RMS_EPS = 1e-6
LN_EPS = 1e-5
GN_EPS = 64e-5
DECAY_C = -0.6065306597126334


class Buf:
    __slots__ = ("name", "last_w", "readers")

    def __init__(self, name=""):
        self.name = name
        self.last_w = None
        self.readers = []


class Op:
    __slots__ = ("eng", "fn", "deps", "signal", "sigval", "is_dma", "dsem", "dval", "prev_same_sem", "epoch")

    def __init__(self, eng, fn, is_dma):
        self.eng = eng
        self.fn = fn
        self.deps = []
        self.signal = False
        self.sigval = 0
        self.is_dma = is_dma
        self.dsem = None
        self.dval = 0
        self.prev_same_sem = None
        self.epoch = 0


class Sched:
    ENGS = ("pe", "dve", "act", "pool", "sp")
    NDSEM = 12
    NQ = {"sp": 12, "pool": 6}

    def __init__(self, nc, stack):
        self.nc = nc
        self.stack = stack
        self.ops = []
        self.eng_obj = {"pe": nc.tensor, "dve": nc.vector, "act": nc.scalar, "pool": nc.gpsimd, "sp": nc.sync}
        self.last_op = {}
        self.dma_since = []
        self.bar_deps = {}
        self.mute = False
        self.cur_epoch = 0
        self.capture = None

    def op(self, eng, fn, reads=(), writes=(), dma=False, strict=False):
        if self.mute:
            return None
        if self.capture is not None:
            self.capture.append((eng, fn, reads, writes, dma, strict))
            return None
        o = Op(eng, fn, dma)
        o.epoch = self.cur_epoch
        deps = set()
        for b in reads:
            w = b.last_w
            if w is not None:
                deps.add(w)
        for b in writes:
            w = b.last_w
            if w is not None and (STRICT or strict or eng in STRICT_ENGS or w.is_dma or dma or w.eng != eng):
                deps.add(w)
            for r in b.readers:
                if STRICT or eng in STRICT_ENGS or r.is_dma or dma or r.eng != eng:
                    deps.add(r)
        if eng in self.bar_deps:
            for d in self.bar_deps.pop(eng):
                deps.add(d)
        o.deps = list(deps)
        for d in o.deps:
            d.signal = True
        for b in reads:
            b.readers.append(o)
        for b in writes:
            b.last_w = o
            b.readers = []
        if dma:
            o.signal = True
            self.dma_since.append(o)
        else:
            self.last_op[eng] = o
        self.ops.append(o)
        return o

    def barrier(self):
        deps = list(self.last_op.values()) + list(self.dma_since)
        self.dma_since = []
        for e in self.ENGS:
            self.bar_deps[e] = list(deps) + self.bar_deps.get(e, [])

    def epoch(self):
        self.ops.append(None)
        self.cur_epoch += 1
        self.last_op = {}
        self.dma_since = []
        self.bar_deps = {}

    def emit(self, final_wait_eng="sp"):
        nc = self.nc
        epochs = [[]]
        for o in self.ops:
            if o is None:
                epochs.append([])
            else:
                epochs[-1].append(o)
        esems = [{e: self.stack.enter_context(nc.semaphore("s%d_%s" % (ei, e))) for e in self.ENGS}
                 for ei in range(len(epochs))]
        dsems = {}
        for q in ("sp", "pool"):
            dsems[q] = [self.stack.enter_context(nc.semaphore("d_%s%d" % (q, i))) for i in range(self.NQ[q])]
        allsems = [x for es in esems for x in es.values()] + [x for q in dsems for x in dsems[q]]

        def clear_all():
            nc.all_engine_barrier()
            for s_ in allsems:
                nc.gpsimd.sem_clear(s_)
            nc.all_engine_barrier()

        nwaits = 0
        maxcnt = 0
        clear_all()
        dcount = {q: [0] * self.NQ[q] for q in dsems}
        dlast = {q: [None] * self.NQ[q] for q in dsems}
        drr = {q: 0 for q in dsems}
        seen_d = {e: {} for e in self.ENGS}
        for ei, ops in enumerate(epochs):
            sem = esems[ei]
            cnt = {e: 0 for e in self.ENGS}
            for o in ops:
                if o.is_dma:
                    q = o.eng
                    i = drr[q]
                    drr[q] = (i + 1) % self.NQ[q]
                    dcount[q][i] += 16
                    o.dsem = dsems[q][i]
                    o.dval = dcount[q][i]
                    o.prev_same_sem = dlast[q][i]
                    dlast[q][i] = o
                elif o.signal:
                    cnt[o.eng] += 1
                    o.sigval = cnt[o.eng]
            maxcnt = max(maxcnt, max(cnt.values()))
            seen = {e: {} for e in self.ENGS}
            for o in ops:
                e = o.eng
                eo = self.eng_obj[e]
                deps = [d for d in o.deps if d.epoch == ei]
                if o.is_dma and o.prev_same_sem is not None:
                    deps.append(o.prev_same_sem)
                for d in deps:
                    if d.is_dma:
                        if seen_d[e].get(id(d.dsem), 0) < d.dval:
                            eo.wait_ge(d.dsem, d.dval)
                            seen_d[e][id(d.dsem)] = d.dval
                            nwaits += 1
                    else:
                        if seen[e].get(d.eng, 0) < d.sigval:
                            eo.wait_ge(sem[d.eng], d.sigval)
                            seen[e][d.eng] = d.sigval
                            nwaits += 1
                inst = o.fn()
                if o.is_dma:
                    inst.then_inc(o.dsem, 16)
                elif o.signal:
                    inst.then_inc(sem[e], 1)
            for q in dsems:
                eo = self.eng_obj[q]
                for i in range(self.NQ[q]):
                    d = dlast[q][i]
                    if d is not None and seen_d[q].get(id(d.dsem), 0) < d.dval:
                        eo.wait_ge(d.dsem, d.dval)
                        seen_d[q][id(d.dsem)] = d.dval
            if ei + 1 < len(epochs):
                nc.all_engine_barrier()
        clear_all()
        self.stats = dict(n_ops=len(self.ops), n_waits=nwaits, maxcnt=maxcnt, n_epochs=len(epochs))


def build(n_seg, npb, n_layers=2):
    nc = bass.Bass("TRN2", target_bir_lowering=False)
    TP = npb * TB
    NT = TP + 128
    NL = n_layers

    def din(name, shape):
        return nc.dram_tensor(name, list(shape), F32, kind="ExternalInput").ap()

    def dout(name, shape):
        return nc.dram_tensor(name, list(shape), F32, kind="ExternalOutput").ap()

    d_xp = din("xTp", [n_seg, 1024, TP])
    d_xs = din("xTs", [1024, 128])
    d_pv = din("pv", [2, 128, NPV])
    d_win = din("w_in", [2, 1024, 2176])
    d_wout = din("w_out", [2, 1024, 1024])
    d_wg = din("wg", [2, 1024, 2816])
    d_wu = din("wu", [2, 1024, 2816])
    d_wd = din("wd", [2, 2816, 1024])
    d_lw = din("lw", [2, 128, 256])
    d_poolw = din("poolw", [2, 2, 128, 128])
    d_cf = din("cf", [128, 8, 128])
    d_m1 = din("m1", [128, 512])
    d_m2 = din("m2", [64, 128])
    d_scan = din("scanm", [128, TB])
    d_icnt = din("icnt", [128, 2, TB])
    d_cconv = din("cconv", [2, 2, 128, 2, 30])
    d_srw = din("srw", [2, 2, 4, 64, 64])
    d_sshift = din("sshift", [2, 2, 128, 7])
    d_cpool = din("cpool", [2, 2, 128, 2, 15])
    d_ckT = din("ckT", [2, 2, 128, 128])
    d_cv = din("cv", [2, 2, 128, 128])

    o_yp = dout("yTp", [n_seg, 1024, TP])
    o_ys = dout("yTs", [1024, 128])
    o_conv = dout("o_conv", [2, 3, 128, 2, 30])
    o_rw = dout("o_rw", [2, 3, 4, 64, 64])
    o_shift = dout("o_shift", [2, 3, 128, 7])
    o_pool = dout("o_pool", [2, 3, 128, 2, 15])
    o_k = dout("o_k", [2, 3, 128, 128])
    o_v = dout("o_v", [2, 3, 128, 128])

    st = contextlib.ExitStack()
    with st:
        S = Sched(nc, st)

        def sb(name, shape, dt):
            return st.enter_context(nc.sbuf_tensor(name, list(shape), dt))

        xT = sb("xT", [128, 8, NT], F32)
        xB = [[Buf() for _ in range(npb + 1)] for _ in range(8)]
        win = sb("win", [128, 8, 2176], BF16)
        winB = [Buf() for _ in range(8)]
        wout = sb("wout", [128, 8, 1024], BF16)
        woutB = Buf()
        pv = sb("pvt", [128, 2, NPV], F32)
        pvB = Buf()
        dv = sb("dvt", [128, 2, 16], F32)
        dvB = Buf()
        cf = sb("cft", [128, 8, 128], F32)
        cfB = Buf()
        cb16 = sb("cb16", [128, 7, 128], BF16)
        cb16B = Buf()
        m1 = sb("m1t", [128, 512], F32)
        mX = m1
        m2 = sb("m2t", [64, 128], F32)
        scanm = sb("scanmt", [128, TB], F32)
        icnt = sb("icntt", [128, 2, TB], F32)
        cmB = Buf()
        lw = sb("lwt", [128, 256], BF16)
        lwB = Buf()
        poolw = sb("poolwt", [128, 2, 128], BF16)
        poolwB = Buf()
        cbP = [sb("cbP%d" % l, [128, 2, 30 + TB], F32) for l in range(2)]
        pbP = [sb("pbP%d" % l, [128, 2, 16 + TB], F32) for l in range(2)]
        kbP = [sb("kbP%d" % l, [128, 128 + TB], BF16) for l in range(2)]
        vbP = [sb("vbP%d" % l, [64, 2 + TB // 64, 2, 128], BF16) for l in range(2)]
        shP = [sb("shP%d" % l, [128, 7], F32) for l in range(2)]
        SsP = [sb("SsP%d" % l, [128, 4, 128], F32) for l in range(2)]
        cbPB = [Buf() for _ in range(2)]
        pbPB = [Buf() for _ in range(2)]
        kbPB = [Buf() for _ in range(2)]
        vbPB = [Buf() for _ in range(2)]
        shPB = [Buf() for _ in range(2)]
        SsPB = [[Buf() for _ in range(4)] for _ in range(2)]
        cbS = [sb("cbS%d" % s, [128, 2, 30 + 64], F32) for s in range(2)]
        pbS = [sb("pbS%d" % s, [128, 2, 16 + 64], F32) for s in range(2)]
        kbS = [sb("kbS%d" % s, [128, 128 + 64], BF16) for s in range(2)]
        vbS = [sb("vbS%d" % s, [64, 3, 2, 128], BF16) for s in range(2)]
        shS = [sb("shS%d" % s, [128, 7], F32) for s in range(2)]
        SsS = [sb("SsS%d" % s, [128, 4, 128], F32) for s in range(2)]
        cbSB = [Buf() for _ in range(2)]
        pbSB = [Buf() for _ in range(2)]
        kbSB = [Buf() for _ in range(2)]
        vbSB = [Buf() for _ in range(2)]
        shSB = [Buf() for _ in range(2)]
        SsSB = [[Buf() for _ in range(4)] for _ in range(2)]
        bnk = sb("bnk", [128, 2, 96], F32)
        bnv = sb("bnv", [128, 2, 128], F32)
        bnkB = [Buf(), Buf()]
        bnvB = [Buf(), Buf()]

        ARENA_F32 = 12416
        arena = sb("arena", [128, ARENA_F32], F32)
        apos = [0]

        def carve(shape, dt, parts=128):
            n = 1
            for s_ in shape:
                n *= s_
            nb = n * (4 if dt == F32 else 2)
            nf = (nb + 3) // 4
            a0 = apos[0]
            apos[0] += nf
            build.amax = max(getattr(build, 'amax', 0), apos[0])
            assert apos[0] <= ARENA_F32, ("arena overflow", apos[0])
            v = arena[0:parts, a0:a0 + nf]
            if dt != F32:
                v = v.bitcast(dt)
            if len(shape) == 1:
                return v
            names = " ".join("d%d" % i for i in range(len(shape)))
            kw = {"d%d" % i: shape[i] for i in range(len(shape))}
            return v.rearrange("p (%s) -> p %s" % (names, names), **kw)

        banks = [st.enter_context(nc.psum_tensor("bank%d" % i, [128, 512], F32)) for i in range(8)]
        bankB = [Buf() for _ in range(8)]
        brr = {"all": 0, "w": 0, "a": 0, "f": 0}
        bpool = ["all"]
        BPOOLS = {"all": list(range(8)), "w": [0, 1, 2, 3], "a": [4, 5], "f": [6, 7]}

        def bank():
            p = bpool[0]
            lst = BPOOLS[p]
            i = lst[brr[p] % len(lst)]
            brr[p] += 1
            return banks[i], bankB[i]

        def V(fn, r, w):
            S.op("dve", fn, r, w)

        def A(fn, r, w):
            S.op("act", fn, r, w)

        def G(fn, r, w):
            S.op("pool", fn, r, w)

        def T(fn, r, w):
            f_, start_ = fn
            S.op("pe", f_, r, w, strict=start_)

        def D(fn, r, w, q="sp"):
            S.op(q, fn, r, w, dma=True)

        def mm(o, l, r, a=True, b=True, order=False):
            return (lambda: nc.tensor.matmul(o, lhsT=l, rhs=r, start=a, stop=b)), (a or order or l.dtype == F32)

        def tr(o, i, ident):
            return (lambda: nc.tensor.transpose(o, i, ident)), True

        def tt(o, a, b, op):
            return lambda: nc.vector.tensor_tensor(out=o, in0=a, in1=b, op=op)

        def ts(o, a, s1, op0, s2=None, op1=None):
            if op1 is None:
                return lambda: nc.vector.tensor_scalar(out=o, in0=a, scalar1=s1, scalar2=None, op0=op0)
            return lambda: nc.vector.tensor_scalar(out=o, in0=a, scalar1=s1, scalar2=s2, op0=op0, op1=op1)

        def stt(o, a, s, b, op0, op1):
            return lambda: nc.vector.scalar_tensor_tensor(out=o, in0=a, scalar=s, in1=b, op0=op0, op1=op1)

        def act(o, i, f, bias=None, scale=None):
            kw = {}
            if bias is not None:
                kw["bias"] = bias
            if scale is not None:
                kw["scale"] = scale
            return lambda: nc.scalar.activation(out=o, in_=i, func=f, **kw)

        def vcopy(o, i):
            return lambda: nc.vector.tensor_copy(out=o, in_=i)

        def gcopy(o, i):
            return lambda: nc.gpsimd.tensor_copy(out=o, in_=i)

        def gmemset(o, v):
            return lambda: nc.gpsimd.memset(o, v)

        def dma(o, i, q="sp"):
            if q == "sp":
                return lambda: nc.sync.dma_start(out=o, in_=i)
            return lambda: nc.gpsimd.dma_start(out=o, in_=i)

        def rsqrt_act(dst, src, eps, rd, wr, tmp=None):
            t_ = dst if tmp is None else tmp
            A(act(t_, src, AF.Ln, bias=eps), rd, wr)
            A(act(dst, t_, AF.Exp, scale=-0.5), wr, wr)

        def sigmoid_le(dst, src, tmp, rd, wr, tmpB, nbias=None, xscale=1.0):
            if nbias is None:
                A(act(tmp, src, AF.Exp, scale=-xscale), rd, [tmpB])
            else:
                A(act(tmp, src, AF.Exp, bias=nbias, scale=-xscale), rd, [tmpB])
            A(act(tmp, tmp, AF.Ln, bias=1.0), [tmpB], [tmpB])
            A(act(dst, tmp, AF.Exp, scale=-1.0), [tmpB], wr)

        D(dma(pv[:], d_pv.rearrange("l p n -> p l n")), [], [pvB])
        D(dma(cf[:], d_cf), [], [cfB])
        D(dma(m1[:], d_m1), [], [cmB])
        D(dma(m2[:], d_m2), [], [cmB])
        D(dma(scanm[:], d_scan), [], [cmB])
        D(dma(icnt[:], d_icnt), [], [cmB])
        IDN = cf[:, 0, :]
        ONES256 = cf[:, 2, :]
        BLKM = cf[:, 3, :]
        BLKS = cf[:, 4, :]
        ISH = cf[:, 6, :]
        IDUP = cf[:, 7, 0:64]
        G(gcopy(cb16[:, 0, :], cf[:, 1, :]), [cfB], [cb16B])
        G(gcopy(cb16[:, 1, :], cf[:, 3, :]), [cfB], [cb16B])
        G(gcopy(cb16[:, 2, :], cf[:, 5, :]), [cfB], [cb16B])
        G(gcopy(cb16[:, 3, :], cf[:, 0, :]), [cfB], [cb16B])
        G(gcopy(cb16[:, 4, :], cf[:, 6, :]), [cfB], [cb16B])
        G(gcopy(cb16[:, 5, :], cf[:, 4, :]), [cfB], [cb16B])
        G(gcopy(cb16[:, 6, :], cf[:, 2, :]), [cfB], [cb16B])
        BLKS_B = cb16[:, 5, :]
        ONES256_B = cb16[:, 6, :]
        IDN_B = cb16[:, 3, :]
        ISH_B = cb16[:, 4, :]
        ONES1024_B = cb16[:, 0, :]
        BLKM_B = cb16[:, 1, :]
        ONES_B = cb16[:, 2, :]
        for l in range(2):
            V(ts(dv[:, l, 0:7], pv[:, l, 84:91], -1.0, ALU.mult, 1.0, ALU.add), [pvB], [dvB])
            V(ts(dv[:, l, 7:9], pv[:, l, 97:99], -1.0, ALU.mult, 1.0, ALU.add), [pvB], [dvB])
            A(act(dv[:, l, 9:11], pv[:, l, 109:111], AF.Exp), [pvB], [dvB])
            V(ts(dv[:, l, 12:16], pv[:, l, 91:95], -1.0, ALU.mult), [pvB], [dvB])

        def PV(l, c, p0=0, p1=128):
            return pv[p0:p1, l, c:c + 1]

        def DVc(l, c, p0=0, p1=128):
            return dv[p0:p1, l, c:c + 1]

        for l in range(2):
            G(gmemset(cbP[l][:, :, 0:30], 0.0), [], [cbPB[l]])
            G(gmemset(pbP[l][:, :, 0:16], 0.0), [], [pbPB[l]])
            G(gmemset(kbP[l][:, 0:128], 0.0), [], [kbPB[l]])
            G(gmemset(vbP[l][:, 0:2], 0.0), [], [vbPB[l]])
            G(gmemset(shP[l][:], 0.0), [], [shPB[l]])
            G(gmemset(SsP[l][:], 0.0), [], SsPB[l])

        def rmsnorm_block(l, gcol0, blk, c0, tb, dst, dstB, sqb, sqbB, rstd, rstdB):
            ps, psB = bank()
            for k in range(8):
                A(act(sqb[:, k % 2, 0:tb], xT[:, k, c0:c0 + tb], AF.Square), [xB[k][blk]], [sqbB[k % 2]])
                T(mm(ps[:, 0:tb], ONES1024_B, sqb[:, k % 2, 0:tb], k == 0, k == 7), [cb16B, sqbB[k % 2]], [psB])
            rsqrt_act(rstd[:, 0:tb], ps[:, 0:tb], RMS_EPS, [psB], [rstdB])
            for k in range(8):
                V(stt(dst(k), xT[:, k, c0:c0 + tb], PV(l, gcol0 + k), rstd[:, 0:tb], ALU.mult, ALU.mult),
                  [xB[k][blk], pvB, rstdB], [dstB(k)])

        def load_segment(seg):
            for g0 in range(0, npb, 4):
                g1 = min(npb, g0 + 4)
                for k in range(8):
                    D(dma(xT[:, k, g0 * TB:g1 * TB], d_xp[seg, k * 128:(k + 1) * 128, g0 * TB:g1 * TB]), [],
                      [xB[k][b] for b in range(g0, g1)])
            if seg == 0:
                for k in range(8):
                    D(dma(xT[:, k, TP:TP + 128], d_xs[k * 128:(k + 1) * 128, :]), [], [xB[k][npb]])

        def store_segment(seg):
            return

        def mixer_phase(seg, l):
            apos[0] = 0
            S.mute = False
            last_seg = (seg == n_seg - 1)
            for k in range(8):
                D(dma(win[:, k, :], d_win[l, k * 128:(k + 1) * 128, :], "pool"), [], [winB[k]], q="pool")
            for k in range(8):
                D(dma(wout[:, k, :], d_wout[l, k * 128:(k + 1) * 128, :], "pool"), [], [woutB], q="pool")
            D(dma(lw[:], d_lw[l], "pool"), [], [lwB], q="pool")
            D(dma(poolw[:], d_poolw[l].rearrange("t p o -> p t o"), "pool"), [], [poolwB], q="pool")
            xnP = [carve([8, TB], BF16) for _ in range(2)]
            xnPB = [[Buf() for _ in range(8)] for _ in range(2)]
            sqb = carve([2, TB], BF16)
            sqbB = [Buf(), Buf()]
            mix = carve([8, TB], BF16)
            mixB = [Buf() for _ in range(8)]
            ub = carve([5, TB], F32)
            ubB = [Buf() for _ in range(5)]
            ubvP = [carve([2, TB], F32) for _ in range(2)]
            ubvPB = [[Buf(), Buf()] for _ in range(2)]
            NSCR = 9
            scrF = [carve([TB], F32) for _ in range(NSCR)]
            scrFB = [Buf() for _ in range(NSCR)]
            scrE = [None] + [carve([TB], F32) for _ in range(5)]
            scrEB = [None] + [Buf() for _ in range(5)]
            AR3cP = [[carve([TB // 64, 3, 64], CH) for _ in range(2)] for _ in range(2)]
            AR3cPB = [[Buf(), Buf()] for _ in range(2)]
            BKcP = [[carve([TB // 64, 2, 64], CH) for _ in range(2)] for _ in range(2)]
            BKcPB = [[Buf(), Buf()] for _ in range(2)]
            BhKhcP = [[carve([2, TB], CH) for _ in range(2)] for _ in range(2)]
            BhKhcPB = [[Buf(), Buf()] for _ in range(2)]
            yb = carve([2, TB], F32)
            ybB = [Buf(), Buf()]
            bonP = [carve([2, TB], F32) for _ in range(2)]
            bonPB = [[Buf(), Buf()] for _ in range(2)]
            gateP = [carve([2, TB], F32) for _ in range(2)]
            gatePB = [[Buf(), Buf()] for _ in range(2)]
            gamP = [carve([2, TB // 64], F32) for _ in range(2)]
            gamPB = [Buf(), Buf()]
            latb = carve([TB], BF16)
            latbB = Buf()
            qn = carve([2, TB], BF16)
            qnB = Buf()
            kf32 = carve([TB], F32)
            kf32B = Buf()
            vf32 = carve([TB // 64, 128], F32, parts=64)
            vf32B = Buf()
            Et = carve([3, 128], BF16, parts=64)
            EtB = Buf()
            rden = carve([128], F32)
            rdenB = Buf()
            dbf = carve([TB], BF16)
            psc = carve([4, 16 + TB], F32)
            pscB = [Buf() for _ in range(4)]
            dbfB = Buf()
            Xw = carve([4, 512], CH)
            XwB = Buf()
            XwhB = [Buf() for _ in range(4)]
            PmhB = [[Buf() for _ in range(4)] for _ in range(2)]
            N2hB = [[Buf() for _ in range(4)] for _ in range(2)]
            W3hB = [Buf() for _ in range(4)]
            F4hB = [Buf() for _ in range(4)]
            N2b = [carve([4, 128], CH) for _ in range(2)]
            N2bB = [Buf(), Buf()]
            Pm = [carve([4, 64], CH) for _ in range(2)]
            PmB = [Buf(), Buf()]
            W3 = carve([4, 128], CH)
            W3B = Buf()
            F4 = carve([2, 4, 128], CH)
            S16 = carve([4, 128], CH)
            S16B = Buf()
            F4B = Buf()
            vdup = carve([4, 128], CH, parts=64)
            vdupB = Buf()

            if seg == 0:
                for s in range(2):
                    D(dma(cbS[s][:, :, 0:30], d_cconv[l, s]), [], [cbSB[s]])
                    D(dma(pbS[s][:, :, 1:16], d_cpool[l, s]), [], [pbSB[s]])
                    D(dma(kbS[s][:, 0:128], d_ckT[l, s], "pool"), [], [kbSB[s]], q="pool")
                    for c in range(2):
                        for dd in range(2):
                            D(dma(vbS[s][:, c, :, dd * 64:(dd + 1) * 64],
                                  d_cv[l, s, c * 64:(c + 1) * 64, :].rearrange("t (h d) -> t h d", h=2), "pool"),
                              [], [vbSB[s]], q="pool")
                    D(dma(shS[s][:], d_sshift[l, s]), [], [shSB[s]])
                    for dd in range(2):
                        D(dma(SsS[s][64:128, :, dd * 64:(dd + 1) * 64], d_srw[l, s].rearrange("h k v -> k h v")),
                          [], SsSB[s])
                    D(dma(bnk[:, s, :], d_ckT[l, s, :, 32:128]), [], [bnkB[s]])
                    D(dma(o_k[l, 1 + s, :, 0:96], bnk[:, s, :]), [bnkB[s]], [])
                    D(dma(bnv[0:96, s, :], d_cv[l, s, 32:128, :]), [], [bnvB[s]])
                    D(dma(o_v[l, 1 + s, 0:96, :], bnv[0:96, s, :]), [bnvB[s]], [])

            blocks = []
            for b in range(npb):
                blocks.append(dict(blk=b, c0=b * TB, tb=TB, sample=False,
                                   pieces=[dict(pc0=0, P=TB, nreal=TB, seq=0, nch=TB // 64,
                                                g0=(seg * npb + b) * (TB // 64))]))
            if seg == 0:
                blocks.append(dict(blk=npb, c0=TP, tb=128, sample=True,
                                   pieces=[dict(pc0=0, P=64, nreal=32, seq=1, nch=1, g0=100),
                                           dict(pc0=64, P=64, nreal=32, seq=2, nch=1, g0=100)]))

            def FE(B, par):
                blk, c0, tb = B["blk"], B["c0"], B["tb"]
                is_last_blk = last_seg and (not B["sample"]) and blk == npb - 1
                xn, xnB = xnP[par], xnPB[par]
                AR3c, AR3cB, BKc, BKcB, BhKhc, BhKhcB = AR3cP[par], AR3cPB[par], BKcP[par], BKcPB[par], BhKhcP[par], BhKhcPB[par]
                gam, gamB, bon, bonB, gate, gateB = gamP[par], gamPB[par], bonP[par], bonPB[par], gateP[par], gatePB[par]
                ubv_, ubvB_ = ubvP[par], ubvPB[par]

                def ubt(ti):
                    if ti in (4, 5):
                        return ubv_[:, ti - 4, :]
                    return ub[:, 4, :] if ti == 6 else ub[:, ti, :]

                def ubtB(ti):
                    if ti in (4, 5):
                        return ubvB_[ti - 4]
                    return ubB[4] if ti == 6 else ubB[ti]
                scr, scrB = scrF, scrFB
                bpool[0] = 'f'
                rmsnorm_block(l, 0, blk, c0, tb, lambda k: xn[:, k, 0:tb], lambda k: xnB[k], sqb, sqbB, scr[0], scrB[0])

                def proj_tile(ti):
                    ps, psB = bank()
                    for k in range(8):
                        T(mm(ps[:, 0:tb], win[:, k, ti * 128:(ti + 1) * 128], xn[:, k, 0:tb], k == 0, k == 7),
                          [winB[k], xnB[k]], [psB])
                    return ps, psB

                def piece_bufs(pc):
                    if pc["seq"] == 0:
                        return cbP[l], cbPB[l], pbP[l], pbPB[l], kbP[l], kbPB[l], vbP[l], vbPB[l], shP[l], shPB[l], SsP[l], SsPB[l]
                    s = pc["seq"] - 1
                    return cbS[s], cbSB[s], pbS[s], pbSB[s], kbS[s], kbSB[s], vbS[s], vbSB[s], shS[s], shSB[s], SsS[s], SsSB[s]

                for ti in range(7):
                    pu, puB = proj_tile(4 + ti)
                    A(act(ubt(ti)[:, 0:tb], pu[:, 0:tb], AF.Copy), [puB], [ubtB(ti)])
                for pc in B["pieces"]:
                    sh, shB = piece_bufs(pc)[8:10]
                    a0, P, nreal = pc["pc0"], pc["P"], pc["nreal"]
                    for ti in range(7):
                        tmp = scr[1 + (ti % 2)]
                        tmpB = scrB[1 + (ti % 2)]
                        A(act(tmp[:, 0:1], sh[:, ti:ti + 1], AF.Copy, scale=PV(l, 84 + ti)), [shB, pvB], [tmpB])
                        A(act(tmp[:, 1:P], ubt(ti)[:, a0:a0 + P - 1], AF.Copy, scale=PV(l, 84 + ti)), [ubtB(ti), pvB], [tmpB])
                        G(gcopy(sh[:, ti:ti + 1], ubt(ti)[:, a0 + nreal - 1:a0 + nreal]), [ubtB(ti), tmpB], [shB])
                        V(stt(ubt(ti)[:, a0:a0 + P], ubt(ti)[:, a0:a0 + P], DVc(l, ti), tmp[:, 0:P], ALU.mult, ALU.add),
                          [ubtB(ti), dvB, tmpB, shB], [ubtB(ti)])
                        if nreal < P and ti < 6:
                            G(gmemset(ubt(ti)[:, a0 + nreal:a0 + P], 0.0), [ubtB(ti)], [ubtB(ti)])
                    if pc["seq"] == 0:
                        if is_last_blk:
                            D(dma(o_shift[l, 0], sh[:]), [shB], [])
                    else:
                        D(dma(o_shift[l, pc["seq"]], sh[:]), [shB], [])
                sigmoid_le(scr[3][0:32, 0:tb], ubt(6)[0:32, 0:tb], scr[3][0:32, 0:tb], [ubtB(6)], [scrB[3]], scrB[3], xscale=2.0)
                A(act(latb[0:32, 0:tb], scr[3][0:32, 0:tb], AF.Copy, bias=-1.0, scale=2.0), [scrB[3]], [latbB])
                A(act(latb[32:64, 0:tb], ubt(6)[32:64, 0:tb], AF.Copy), [ubtB(6)], [latbB])
                sigmoid_le(latb[64:128, 0:tb], ubt(6)[64:128, 0:tb], scr[3][64:128, 0:tb], [ubtB(6)], [latbB], scrB[3])
                nchb = tb // 64
                for ct in range(2):
                    r_, k_, v_ = ubt(ct)[:, 0:tb], ubt(2 + ct)[:, 0:tb], ubt(4 + ct)[:, 0:tb]
                    rB, kB, vB = ubtB(ct), ubtB(2 + ct), ubtB(4 + ct)
                    ld, ldB = scr[1][:, 0:tb], scrB[1]
                    Lc, LcB = scr[2][:, 0:tb], scrB[2]
                    ar, arB = scr[3][:, 0:tb], scrB[3]
                    kkn, kknB = scr[4][:, 0:tb], scrB[4]
                    kmod, kmodB = scr[5][:, 0:tb], scrB[5]
                    bvec, bvecB = scr[6][:, 0:tb], scrB[6]
                    t1, t1B = scr[7][:, 0:tb], scrB[7]
                    t2, t2B = scr[8][:, 0:tb], scrB[8]
                    pw, pwB = bank()
                    T(mm(pw[:, 0:tb], lw[0:32, ct * 128:(ct + 1) * 128], latb[0:32, 0:tb]), [lwB, latbB], [pwB])
                    sigmoid_le(ld, pw[:, 0:tb], ld, [pwB, dvB], [ldB], ldB, nbias=DVc(l, 12 + ct))
                    pa, paB = bank()
                    T(mm(pa[:, 0:tb], lw[32:64, ct * 128:(ct + 1) * 128], latb[32:64, 0:tb]), [lwB, latbB], [paB])
                    sigmoid_le(ar, pa[:, 0:tb], ar, [paB, dvB], [arB], arB, nbias=DVc(l, 14 + ct))
                    pg, pgB = bank()
                    T(mm(pg[:, 0:tb], lw[64:128, ct * 128:(ct + 1) * 128], latb[64:128, 0:tb]), [lwB, latbB], [pgB])
                    A(act(gate[:, ct, 0:tb], pg[:, 0:tb], AF.Copy), [pgB], [gateB[ct]])
                    A(act(ld, ld, AF.Copy, scale=DECAY_C), [ldB], [ldB])
                    if B["sample"]:
                        for pc in B["pieces"]:
                            G(gmemset(scr[1][:, pc["pc0"] + pc["nreal"]:pc["pc0"] + pc["P"]], 0.0), [ldB], [ldB])
                    V(lambda o=Lc, d0=scanm[:, 0:tb], d1=ld: nc.vector.tensor_tensor_scan(
                        out=o, data0=d0, data1=d1, initial=0.0, op0=ALU.mult, op1=ALU.add), [cmB, ldB], [LcB])
                    A(act(t1, k_, AF.Copy, scale=PV(l, 95 + ct)), [kB, pvB], [t1B])
                    t2h = scr[8][:, 0:tb].bitcast(BF16)[:, 0:tb]
                    A(act(t2h, t1, AF.Square), [t1B], [t2B])
                    pss, pssB = bank()
                    T(mm(pss[:, 0:tb], BLKS_B, t2h), [cb16B, t2B], [pssB])
                    rsqrt_act(t2, pss[:, 0:tb], 1e-12, [pssB], [t2B])
                    V(tt(kkn, t1, t2, ALU.mult), [t1B, t2B], [kknB])
                    A(act(t1, ar, AF.Identity, bias=DVc(l, 7 + ct), scale=PV(l, 97 + ct)), [arB, pvB, dvB], [t1B])
                    V(tt(kmod, k_, t1, ALU.mult), [kB, t1B], [kmodB])
                    V(tt(bvec, kkn, ar, ALU.mult), [kknB, arB], [bvecB])
                    t1h = scr[7][:, 0:tb].bitcast(BF16)[:, 0:tb]
                    V(stt(t1h, r_, PV(l, 99 + ct), kmod, ALU.mult, ALU.mult), [rB, pvB, kmodB], [t1B])
                    pbn, pbnB = bank()
                    T(mm(pbn[:, 0:tb], BLKS_B, t1h), [cb16B, t1B], [pbnB])
                    V(tt(bon[:, ct, 0:tb], pbn[:, 0:tb], v_, ALU.mult), [pbnB, vB], [bonB[ct]])
                    A(act(t1, Lc, AF.Exp), [LcB], [t1B])
                    V(tt(AR3c[ct][:, 0:nchb, 1, :], r_.rearrange("p (c t) -> p c t", t=64), t1.rearrange("p (c t) -> p c t", t=64),
                         ALU.mult), [rB, t1B], [AR3cB[ct]])
                    A(act(gam[:, ct, 0:nchb], scr[2][:, 0:tb].rearrange("p (c t) -> p c t", t=64)[:, :, 63], AF.Exp),
                      [LcB], [gamB])
                    V(tt(t2, Lc, ld, ALU.subtract), [LcB, ldB], [t2B])
                    A(act(t2, t2, AF.Exp), [t2B], [t2B])
                    V(stt(AR3c[ct][:, 0:nchb, 0, :], kkn.rearrange("p (c t) -> p c t", t=64), -1.0,
                          t2.rearrange("p (c t) -> p c t", t=64), ALU.mult, ALU.mult), [kknB, t2B], [AR3cB[ct]])
                    A(act(t1, Lc, AF.Exp, scale=-1.0), [LcB], [t1B])
                    V(tt(BKc[ct][:, 0:nchb, 0, :], bvec.rearrange("p (c t) -> p c t", t=64), t1.rearrange("p (c t) -> p c t", t=64),
                         ALU.mult), [bvecB, t1B], [BKcB[ct]])
                    V(tt(BKc[ct][:, 0:nchb, 1, :], kmod.rearrange("p (c t) -> p c t", t=64), t1.rearrange("p (c t) -> p c t", t=64),
                         ALU.mult), [kmodB, t1B], [BKcB[ct]])
                    V(tt(t2.rearrange("p (c t) -> p c t", t=64),
                         scr[2][:, 0:tb].rearrange("p (c t) -> p c t", t=64)[:, :, 63:64].to_broadcast([128, nchb, 64]),
                         Lc.rearrange("p (c t) -> p c t", t=64), ALU.subtract), [LcB], [t2B])
                    A(act(t2, t2, AF.Exp), [t2B], [t2B])
                    V(tt(BhKhc[ct][:, 0, 0:tb], bvec, t2, ALU.mult), [bvecB, t2B], [BhKhcB[ct]])
                    V(tt(BhKhc[ct][:, 1, 0:tb], kmod, t2, ALU.mult), [kmodB, t2B], [BhKhcB[ct]])
                    for j in range(nchb):
                        A(act(AR3c[ct][:, j, 2, :], IDUP, AF.Copy, scale=gam[:, ct, j:j + 1]), [cfB, gamB], [AR3cB[ct]])

            def BE(B, par):
                blk, c0, tb = B["blk"], B["c0"], B["tb"]
                is_last_blk = last_seg and (not B["sample"]) and blk == npb - 1
                xn, xnB = xnP[par], xnPB[par]
                AR3c, AR3cB, BKc, BKcB, BhKhc, BhKhcB = AR3cP[par], AR3cPB[par], BKcP[par], BKcPB[par], BhKhcP[par], BhKhcPB[par]
                gam, gamB, bon, bonB, gate, gateB = gamP[par], gamPB[par], bonP[par], bonPB[par], gateP[par], gatePB[par]
                ubv_, ubvB_ = ubvP[par], ubvPB[par]

                def ubt(ti):
                    if ti in (4, 5):
                        return ubv_[:, ti - 4, :]
                    return ub[:, 4, :] if ti == 6 else ub[:, ti, :]

                def ubtB(ti):
                    if ti in (4, 5):
                        return ubvB_[ti - 4]
                    return ubB[4] if ti == 6 else ubB[ti]
                scr, scrB = scrE, scrEB
                def proj_tile(ti):
                    ps, psB = bank()
                    for k in range(8):
                        T(mm(ps[:, 0:tb], win[:, k, ti * 128:(ti + 1) * 128], xn[:, k, 0:tb], k == 0, k == 7),
                          [winB[k], xnB[k]], [psB])
                    return ps, psB

                def piece_bufs(pc):
                    if pc["seq"] == 0:
                        return cbP[l], cbPB[l], pbP[l], pbPB[l], kbP[l], kbPB[l], vbP[l], vbPB[l], shP[l], shPB[l], SsP[l], SsPB[l]
                    s = pc["seq"] - 1
                    return cbS[s], cbSB[s], pbS[s], pbSB[s], kbS[s], kbSB[s], vbS[s], vbSB[s], shS[s], shSB[s], SsS[s], SsSB[s]

                outer_cap = S.capture
                S.capture = []
                bpool[0] = 'a'
                acc, accB = scr[1:3], scrB[1:3]
                cen, cenB = scr[3:5], scrB[3:5]
                for ct in range(2):
                    pval, pvalB = proj_tile(ct)
                    pgt, pgtB = proj_tile(2 + ct)
                    sigmoid_le(scr[5][:, 0:tb], pgt[:, 0:tb], scr[5][:, 0:tb], [pgtB], [scrB[5]], scrB[5])
                    for pc in B["pieces"]:
                        cb, cbB = piece_bufs(pc)[0:2]
                        a0, P = pc["pc0"], pc["P"]
                        V(tt(cb[:, ct, 30:30 + P], pval[:, a0:a0 + P], scr[5][:, a0:a0 + P], ALU.mult),
                          [pvalB, scrB[5]], [cbB])
                        V(ts(acc[ct][:, a0:a0 + P], cb[:, ct, 0:P], PV(l, 16 + ct * 31), ALU.mult, PV(l, 78 + ct), ALU.add),
                          [cbB, pvB], [accB[ct]])
                        for j in range(1, 31):
                            V(stt(acc[ct][:, a0:a0 + P], cb[:, ct, j:j + P], PV(l, 16 + ct * 31 + j), acc[ct][:, a0:a0 + P],
                                  ALU.mult, ALU.add), [cbB, pvB, accB[ct]], [accB[ct]])
                pm, pmB = bank()
                for ct in range(2):
                    T(mm(pm[:, 0:tb], ONES256, acc[ct][:, 0:tb], ct == 0, ct == 1), [cfB, accB[ct]], [pmB])
                for ct in range(2):
                    V(tt(cen[ct][:, 0:tb], acc[ct][:, 0:tb], pm[:, 0:tb], ALU.subtract), [accB[ct], pmB], [cenB[ct]])
                    A(act(acc[ct][:, 0:tb].bitcast(BF16)[:, 0:tb], cen[ct][:, 0:tb], AF.Square), [cenB[ct]], [accB[ct]])
                pvv, pvvB = bank()
                for ct in range(2):
                    T(mm(pvv[:, 0:tb], ONES256_B, acc[ct][:, 0:tb].bitcast(BF16)[:, 0:tb], ct == 0, ct == 1), [cb16B, accB[ct]], [pvvB])
                rsqrt_act(scr[5][:, 0:tb], pvv[:, 0:tb], LN_EPS, [pvvB], [scrB[5]])
                for ct in range(2):
                    V(tt(cen[ct][:, 0:tb], cen[ct][:, 0:tb], scr[5][:, 0:tb], ALU.mult), [cenB[ct], scrB[5]], [cenB[ct]])
                for ct in range(2):
                    A(act(acc[ct][:, 0:tb], cen[ct][:, 0:tb], AF.Identity, bias=PV(l, 82 + ct), scale=PV(l, 80 + ct)),
                      [cenB[ct], pvB], [accB[ct]])
                    sigmoid_le(scr[5][:, 0:tb], acc[ct][:, 0:tb], scr[5][:, 0:tb], [accB[ct]], [scrB[5]], scrB[5])
                    V(tt(mix[:, ct, 0:tb], acc[ct][:, 0:tb], scr[5][:, 0:tb], ALU.mult), [accB[ct], scrB[5]], [mixB[ct]])
                for pc in B["pieces"]:
                    cb, cbB = piece_bufs(pc)[0:2]
                    if pc["seq"] == 0:
                        G(gcopy(cb[:, :, 0:30], cb[:, :, TB:TB + 30]), [cbB], [cbB])
                        if is_last_blk:
                            D(dma(o_conv[l, 0], cb[:, :, 0:30]), [cbB], [])
                    else:
                        D(dma(o_conv[l, pc["seq"]], cb[:, :, 32:62]), [cbB], [])
                for ct in range(2):
                    pu, puB = proj_tile(11 + ct)
                    for pc in B["pieces"]:
                        pb, pbB = piece_bufs(pc)[2:4]
                        a0, P = pc["pc0"], pc["P"]
                        E_ = 16 + P
                        A(act(pb[:, ct, 16:16 + P], pu[:, a0:a0 + P], AF.Copy), [puB], [pbB])
                        ext = pb[:, ct, :]
                        s2, s4, s8, s16 = psc[:, 0, :], psc[:, 1, :], psc[:, 2, :], psc[:, 3, :]
                        V(tt(s2[:, 2:E_], ext[:, 2:E_], ext[:, 1:E_ - 1], ALU.add), [pbB], [pscB[0]])
                        V(tt(s4[:, 4:E_], s2[:, 4:E_], s2[:, 2:E_ - 2], ALU.add), [pscB[0]], [pscB[1]])
                        if ct == 1:
                            V(tt(s8[:, 8:E_], s4[:, 8:E_], s4[:, 4:E_ - 4], ALU.add), [pscB[1]], [pscB[2]])
                            V(tt(s16[:, 16:E_], s8[:, 16:E_], s8[:, 8:E_ - 8], ALU.add), [pscB[2]], [pscB[3]])
                            srcs = [(s8, pscB[2], 0.125), (s16, pscB[3], 0.0625)]
                        else:
                            srcs = [(s2, pscB[0], 0.5), (s4, pscB[1], 0.25)]
                        first = (pc["seq"] == 0 and seg == 0 and blk == 0)
                        for hf in range(2):
                            p0, p1 = hf * 64, hf * 64 + 64
                            sw, swB, iw = srcs[hf]
                            if first:
                                V(tt(scr[5][p0:p1, 0:P], sw[p0:p1, 16:E_], icnt[p0:p1, ct, 0:P], ALU.mult),
                                  [swB, cmB], [scrB[5]])
                                V(tt(dbf[p0:p1, a0:a0 + P], scr[5][p0:p1, 0:P], ext[p0:p1, 16:E_], ALU.subtract),
                                  [scrB[5], pbB], [dbfB])
                            else:
                                V(stt(dbf[p0:p1, a0:a0 + P], sw[p0:p1, 16:E_], iw, ext[p0:p1, 16:E_], ALU.mult, ALU.subtract),
                                  [swB, pbB], [dbfB])
                    py, pyB = bank()
                    T(mm(py[:, 0:tb], poolw[:, ct, :], dbf[:, 0:tb]), [poolwB, dbfB], [pyB])
                    A(act(mix[:, 4 + ct, 0:tb], py[:, 0:tb], AF.Copy, scale=PV(l, 105 + ct)), [pyB, pvB], [mixB[4 + ct]])
                for pc in B["pieces"]:
                    pb, pbB = piece_bufs(pc)[2:4]
                    if pc["seq"] == 0:
                        G(gcopy(pb[:, :, 1:16], pb[:, :, TB + 1:TB + 16]), [pbB], [pbB])
                        if is_last_blk:
                            D(dma(o_pool[l, 0], pb[:, :, 1:16]), [pbB], [])
                    else:
                        D(dma(o_pool[l, pc["seq"]], pb[:, :, 33:48]), [pbB], [])
                for qi in range(3):
                    pq, pqB = proj_tile(13 + qi)
                    A(act(scr[1][:, 0:tb].bitcast(BF16)[:, 0:tb], pq[:, 0:tb], AF.Square), [pqB], [scrB[1]])
                    pss, pssB = bank()
                    T(mm(pss[:, 0:tb], BLKM_B, scr[1][:, 0:tb].bitcast(BF16)[:, 0:tb]), [cb16B, scrB[1]], [pssB])
                    rsqrt_act(scr[2][:, 0:tb], pss[:, 0:tb], RMS_EPS, [pssB], [scrB[2]])
                    if qi < 2:
                        V(stt(qn[:, qi, 0:tb], pq[:, 0:tb], PV(l, 107), scr[2][:, 0:tb], ALU.mult, ALU.mult),
                          [pqB, pvB, scrB[2]], [qnB])
                    else:
                        V(stt(kf32[:, 0:tb], pq[:, 0:tb], PV(l, 108), scr[2][:, 0:tb], ALU.mult, ALU.mult),
                          [pqB, pvB, scrB[2]], [kf32B])
                for pc in B["pieces"]:
                    kb, kbB, vb, vbB = piece_bufs(pc)[4:8]
                    a0, P, nch = pc["pc0"], pc["P"], pc["nch"]
                    G(gcopy(kb[:, 128:128 + P], kf32[:, a0:a0 + P]), [kf32B], [kbB])
                    for j in range(nch):
                        pvt, pvtB = bank()
                        cj = a0 + j * 64
                        for k in range(8):
                            T(mm(pvt[0:64, 0:128], xn[:, k, cj:cj + 64], win[:, k, 2048:2176], k == 0, k == 7),
                              [xnB[k], winB[k]], [pvtB])
                        jj = (cj // 64)
                        A(act(vf32[:, jj, :], pvt[0:64, 0:128], AF.Copy), [pvtB], [vf32B])
                        V(vcopy(vb[:, 2 + j, :, :].rearrange("t h (a d) -> t h a d", a=2),
                                vf32[:, jj, :].rearrange("t (h d) -> t h d", h=2).unsqueeze(2).to_broadcast([64, 2, 2, 64])),
                          [vf32B], [vbB])
                    if pc["seq"] == 0:
                        if is_last_blk:
                            D(dma(o_k[l, 0], kf32[:, TB - 128:TB]), [kf32B], [])
                            D(dma(o_v[l, 0, 0:64, :], vf32[:, TB // 64 - 2, :]), [vf32B], [])
                            D(dma(o_v[l, 0, 64:128, :], vf32[:, TB // 64 - 1, :]), [vf32B], [])
                    else:
                        D(dma(o_k[l, pc["seq"], :, 96:128], kf32[:, a0:a0 + 32]), [kf32B], [])
                        D(dma(o_v[l, pc["seq"], 96:128, :], vf32[0:32, a0 // 64, :]), [vf32B], [])
                    for j in range(nch):
                        if pc["seq"] == 0:
                            nq = 64
                            keys = [(64 * (j + m_), j + m_, 64) for m_ in range(3) if pc["g0"] + j - 2 + m_ >= 0]
                        else:
                            nq = 32
                            keys = [(0, 0, 64), (64, 1, 64), (128, 2, 32)]
                        q0 = a0 + j * 64
                        for h in range(2):
                            ph = 64 * h
                            pS, pSB = bank()
                            for m_, (koff, vidx, nk) in enumerate(keys):
                                T(mm(pS[0:nk, m_ * 128:m_ * 128 + 2 * nq], kb[ph:ph + 64, koff:koff + nk],
                                     qn[ph:ph + 64, :, q0:q0 + nq]), [kbB, qnB], [pSB])
                                A(act(Et[0:nk, m_, 0:2 * nq], pS[0:nk, m_ * 128:m_ * 128 + 2 * nq], AF.Exp, scale=0.125),
                                  [pSB], [EtB])
                            pN, pNB = bank()
                            nk_ = len(keys)
                            for m_, (koff, vidx, nk) in enumerate(keys):
                                T(mm(pN[:, 0:2 * nq], vb[0:nk, vidx, h, :], Et[0:nk, m_, 0:2 * nq], m_ == 0, m_ == nk_ - 1),
                                  [vbB, EtB], [pNB])
                            for m_, (koff, vidx, nk) in enumerate(keys):
                                T(mm(pN[:, 256:256 + 2 * nq], ONES_B[0:nk, :], Et[0:nk, m_, 0:2 * nq], m_ == 0, m_ == nk_ - 1),
                                  [cb16B, EtB], [pNB])
                            A(act(rden[:, 0:2 * nq], pN[:, 256:256 + 2 * nq], AF.Ln, bias=DVc(l, 9 + h)), [pNB, dvB], [rdenB])
                            A(act(rden[:, 0:2 * nq], rden[:, 0:2 * nq], AF.Exp, scale=-1.0), [rdenB], [rdenB])
                            for g in range(2):
                                p0, p1 = 64 * g, 64 * g + 64
                                V(tt(mix[p0:p1, 6 + h, q0:q0 + nq], pN[p0:p1, g * nq:(g + 1) * nq],
                                     rden[p0:p1, g * nq:(g + 1) * nq], ALU.mult), [pNB, rdenB], [mixB[6 + h]])
                    if pc["seq"] == 0:
                        G(gcopy(kb[:, 0:128], kb[:, TB:TB + 128]), [kbB], [kbB])
                        G(gcopy(vb[:, 0:2], vb[:, TB // 64:TB // 64 + 2]), [vbB], [vbB])

                acd_ops = S.capture
                S.capture = []
                bpool[0] = 'w'
                chs = []
                for pc in B["pieces"]:
                    Ss_, SsB_ = piece_bufs(pc)[10:12]
                    for jl in range(pc["nch"]):
                        chs.append((pc["pc0"] // 64 + jl, Ss_, SsB_))
                assert [c_[0] for c_ in chs] == [0, 1]
                hd = []
                for h4 in range(4):
                    ct, hh = h4 // 2, h4 % 2
                    ph = 64 * hh
                    hd.append((ct, hh, ph, AR3c[ct], AR3cB[ct], BKc[ct], BKcB[ct], BhKhc[ct], BhKhcB[ct],
                               cb16[ph:ph + 64, 3, ph:ph + 64]))
                P1 = [bank() for _ in range(4)]
                for step in range(6):
                    for w in range(2):
                        q0, q1, j, cj = 64 * w, 64 * w + 64, w, 64 * w
                        for h4 in range(4):
                            ct, hh, ph, A3, A3B, BK_, BK_B, BH, BHB, idb = hd[h4]
                            p1, p1B = P1[h4]
                            if step == 0:
                                T(mm(p1[q0:q1, 0:128], BK_[ph:ph + 64, j, 0, :], A3[ph:ph + 64, j, 0:2, :]), [BK_B, A3B], [p1B])
                            elif step == 1:
                                T(mm(p1[q0:q1, 128:192], BH[ph:ph + 64, 0, cj:cj + 64], idb), [BHB, cb16B], [p1B])
                            elif step == 2:
                                T(mm(p1[q0:q1, 192:256], BK_[ph:ph + 64, j, 1, :], A3[ph:ph + 64, j, 1, :]), [BK_B, A3B], [p1B])
                            elif step == 3:
                                T(mm(p1[q0:q1, 256:320], BH[ph:ph + 64, 1, cj:cj + 64], idb), [BHB, cb16B], [p1B])
                            elif step == 4:
                                T(mm(p1[q0:q1, 320:448], A3[ph:ph + 64, j, 0, :], BK_[ph:ph + 64, j, :, :]), [BK_B, A3B], [p1B])
                            else:
                                T(mm(p1[q0:q1, 448:512], A3[ph:ph + 64, j, 0, :], idb), [A3B, cb16B], [p1B])
                for h4 in range(4):
                    p1, p1B = P1[h4]
                    V(tt(Xw[:, h4, :], p1[:, 0:512], mX[:, :], ALU.mult), [p1B, cmB], [XwhB[h4]])
                V(tt(Pm[0][:, :, :], Xw[:, :, 0:64], IDUP.unsqueeze(1).to_broadcast([128, 4, 64]), ALU.add),
                  XwhB + [cfB], PmhB[0])
                Nsrc, NTsrc, NsrcB = (lambda h4: Xw[:, h4, 0:64]), (lambda h4: Xw[:, h4, 320:384]), (lambda h4: [XwhB[h4]])
                pcur = 0
                for r in range(5):
                    PN = [bank() for _ in range(4)]
                    nb_ = N2b[r % 2]
                    for half in range(2):
                        if r == 4 and half == 0:
                            continue
                        for w in range(2):
                            q0, q1 = 64 * w, 64 * w + 64
                            for h4 in range(4):
                                pn, pnB = PN[h4]
                                if half == 0:
                                    T(mm(pn[q0:q1, 0:64], NTsrc(h4)[q0:q1, :], Nsrc(h4)[q0:q1, :]), NsrcB(h4), [pnB])
                                else:
                                    T(mm(pn[q0:q1, 64:128], Nsrc(h4)[q0:q1, :], NTsrc(h4)[q0:q1, :]), NsrcB(h4), [pnB])
                    c0_ = 64 if r == 4 else 0
                    for h4 in range(4):
                        pn, pnB = PN[h4]
                        A(act(nb_[:, h4, c0_:128], pn[:, c0_:128], AF.Copy), [pnB], [N2hB[r % 2][h4]])
                    PP = [bank() for _ in range(4)]
                    for w in range(2):
                        q0, q1 = 64 * w, 64 * w + 64
                        for h4 in range(4):
                            pp, ppB = PP[h4]
                            T(mm(pp[q0:q1, 0:64], nb_[q0:q1, h4, 64:128], Pm[pcur][q0:q1, h4, :]),
                              [N2hB[r % 2][h4], PmhB[pcur][h4]], [ppB])
                    for h4 in range(4):
                        pp, ppB = PP[h4]
                        V(tt(Pm[1 - pcur][:, h4, :], Pm[pcur][:, h4, :], pp[:, 0:64], ALU.add),
                          [PmhB[pcur][h4], ppB], [PmhB[1 - pcur][h4]])
                    pcur = 1 - pcur
                    Nsrc = (lambda h4, nb_=nb_: nb_[:, h4, 0:64])
                    NTsrc = (lambda h4, nb_=nb_: nb_[:, h4, 64:128])
                    NsrcB = (lambda h4, r=r: [N2hB[r % 2][h4]])
                P3 = [bank() for _ in range(4)]
                for w in range(2):
                    q0, q1 = 64 * w, 64 * w + 64
                    for h4 in range(4):
                        p3, p3B = P3[h4]
                        T(mm(p3[q0:q1, 0:128], Pm[pcur][q0:q1, h4, :], Xw[q0:q1, h4, 384:512]), [PmhB[pcur][h4], XwhB[h4]], [p3B])
                for h4 in range(4):
                    p3, p3B = P3[h4]
                    A(act(W3[:, h4, :], p3[:, 0:128], AF.Copy), [p3B], [W3hB[h4]])
                P4 = [bank() for _ in range(4)]
                for w in range(2):
                    q0, q1, j = 64 * w, 64 * w + 64, w
                    for step in range(3):
                        for h4 in range(4):
                            ct, hh = h4 // 2, h4 % 2
                            ph = 64 * hh
                            p4, p4B = P4[h4]
                            o4 = p4[:, 128 * w:128 * w + 128]
                            if step == 0:
                                T(mm(o4, W3[q0:q1, h4, :], Xw[q0:q1, h4, 64:192], True, False), [W3hB[h4], XwhB[h4]], [p4B])
                            elif step == 1:
                                i0_ = IDN_B[0:64, :] if w == 0 else ISH_B[64:128, :]
                                T(mm(o4, i0_, Xw[q0:q1, h4, 192:320], False, False), [cb16B, XwhB[h4]], [p4B])
                            else:
                                ish = ISH_B[0:64, :] if hh == 0 else IDN_B[64:128, :]
                                T(mm(o4, ish, AR3c[ct][ph:ph + 64, j, 1:3, :], False, True, order=True), [cb16B, AR3cB[ct]], [p4B])
                for h4 in range(4):
                    p4, p4B = P4[h4]
                    src4 = p4[:, 0:256].rearrange("p (w c) -> p w c", w=2)
                    A(act(F4[:, :, h4, :], src4, AF.Copy), [p4B], [F4hB[h4]])
                for (w, Ss, SsB) in chs:
                    cj = 64 * w
                    pV, pVB = bank()
                    for ct in range(2):
                        T(tr(pV[0:64, ct * 128:(ct + 1) * 128], ubt(4 + ct)[:, cj:cj + 64], IDN), [ubtB(4 + ct), cfB], [pVB])
                    V(vcopy(vdup[:, :, :].rearrange("t h (a d) -> t h a d", a=2),
                            pV[0:64, 0:256].rearrange("t (h d) -> t h d", h=4).unsqueeze(2).to_broadcast([64, 4, 2, 64])),
                      [pVB], [vdupB])
                    A(act(S16[64:128, :, :], Ss[64:128, :, :], AF.Copy), SsB, [S16B])
                    PY = [bank() for _ in range(4)]
                    for step in range(2):
                        for h4 in range(4):
                            pY, pYB = PY[h4]
                            oY = pY[:, 0:64]
                            if step == 0:
                                T(mm(oY, S16[64:128, h4, :], F4[64:128, w, h4, 0:64], True, False), [S16B, F4hB[h4]], [pYB])
                            else:
                                T(mm(oY, vdup[:, h4, :], F4[0:64, w, h4, 0:64], False, True, order=True), [vdupB, F4hB[h4]], [pYB])
                    for h4 in range(4):
                        ct, hh = h4 // 2, h4 % 2
                        ph = 64 * hh
                        pY, pYB = PY[h4]
                        oY = pY[:, 0:64]
                        A(act(yb[ph:ph + 64, ct, cj:cj + 64], oY[ph:ph + 64, :], AF.Copy), [pYB], [ybB[ct]])
                    PS_ = [bank() for _ in range(4)]
                    for step in range(2):
                        for h4 in range(4):
                            pSb, pSbB = PS_[h4]
                            if step == 0:
                                T(mm(pSb[:, 0:128], F4[64:128, w, h4, :], S16[64:128, h4, :], True, False), [S16B, F4hB[h4]], [pSbB])
                            else:
                                T(mm(pSb[:, 0:128], F4[0:64, w, h4, :], vdup[:, h4, :], False, True, order=True), [vdupB, F4hB[h4]], [pSbB])
                    for h4 in range(4):
                        pSb, pSbB = PS_[h4]
                        A(act(Ss[64:128, h4, :], pSb[64:128, 0:128], AF.Copy), [pSbB], [SsB[h4]])
                wv_ops = S.capture
                S.capture = outer_cap
                bpool[0] = 'a'
                ia = iw = 0
                GA, GW = 3, 4
                while ia < len(acd_ops) or iw < len(wv_ops):
                    for r_ in wv_ops[iw:iw + GW]:
                        S.op(r_[0], r_[1], r_[2], r_[3], dma=r_[4], strict=r_[5])
                    iw += GW
                    for r_ in acd_ops[ia:ia + GA]:
                        S.op(r_[0], r_[1], r_[2], r_[3], dma=r_[4], strict=r_[5])
                    ia += GA
                for pc in B["pieces"]:
                    Ss, SsB = piece_bufs(pc)[10:12]
                    if pc["seq"] == 0:
                        if is_last_blk:
                            D(dma(o_rw[l, 0].rearrange("h k v -> k h v"), Ss[64:128, :, 0:64]), SsB, [])
                    else:
                        D(dma(o_rw[l, pc["seq"]].rearrange("h k v -> k h v"), Ss[64:128, :, 0:64]), SsB, [])
                for ct in range(2):
                    pmn, pmnB = bank()
                    T(mm(pmn[:, 0:tb], BLKM, yb[:, ct, 0:tb]), [cfB, ybB[ct]], [pmnB])
                    c_, c_B = scr[1][:, 0:tb], scrB[1]
                    s_, s_B = scr[2][:, 0:tb], scrB[2]
                    V(tt(c_, yb[:, ct, 0:tb], pmn[:, 0:tb], ALU.subtract), [ybB[ct], pmnB], [c_B])
                    s_h = scr[2][:, 0:tb].bitcast(BF16)[:, 0:tb]
                    A(act(s_h, c_, AF.Square), [c_B], [s_B])
                    pvr, pvrB = bank()
                    T(mm(pvr[:, 0:tb], BLKM_B, s_h), [cb16B, s_B], [pvrB])
                    rsqrt_act(s_, pvr[:, 0:tb], GN_EPS, [pvrB], [s_B])
                    V(tt(c_, c_, s_, ALU.mult), [c_B, s_B], [c_B])
                    A(act(c_, c_, AF.Identity, bias=PV(l, 103 + ct), scale=PV(l, 101 + ct)), [c_B, pvB], [c_B])
                    V(tt(c_, c_, bon[:, ct, 0:tb], ALU.add), [c_B, bonB[ct]], [c_B])
                    V(tt(mix[:, 2 + ct, 0:tb], c_, gate[:, ct, 0:tb], ALU.mult), [c_B, gateB[ct]], [mixB[2 + ct]])

                for ot in range(8):
                    po, poB = bank()
                    for k in range(8):
                        T(mm(po[:, 0:tb], wout[:, k, ot * 128:(ot + 1) * 128], mix[:, k, 0:tb], k == 0, k == 7),
                          [woutB, mixB[k]], [poB])
                    V(tt(xT[:, ot, c0:c0 + tb], xT[:, ot, c0:c0 + tb], po[:, 0:tb], ALU.add), [xB[ot][blk], poB], [xB[ot][blk]])

            def cap(fn_, *a_):
                S.capture = []
                fn_(*a_)
                ops_ = S.capture
                S.capture = None
                return ops_

            def replay(lst):
                for r_ in lst:
                    S.op(r_[0], r_[1], r_[2], r_[3], dma=r_[4], strict=r_[5])

            replay(cap(FE, blocks[0], 0))
            for i_, B in enumerate(blocks):
                be_ops = cap(BE, B, i_ % 2)
                fe_ops = cap(FE, blocks[i_ + 1], (i_ + 1) % 2) if i_ + 1 < len(blocks) else []
                nb_, nf_ = len(be_ops), len(fe_ops)
                ib_ = if_ = 0
                GB = 6
                GF = max(1, -(-GB * nf_ // max(nb_, 1)))
                while ib_ < nb_ or if_ < nf_:
                    replay(be_ops[ib_:ib_ + GB])
                    ib_ += GB
                    replay(fe_ops[if_:if_ + GF])
                    if_ += GF
                if B["blk"] == npb // 2 - 1:
                    S.epoch()
            bpool[0] = 'all'

        def ffn_phase(seg, l):
            apos[0] = 0
            S.mute = 'F' not in DBG
            FS = 256
            nsl = 2816 // FS
            wgs = [carve([8, FS], BF16) for _ in range(2)]
            wus = [carve([8, FS], BF16) for _ in range(2)]
            wds = [carve([2, 1024], BF16) for _ in range(2)]
            wsB = [[Buf(), Buf(), Buf()] for _ in range(2)]
            sqb = carve([2, TB], BF16)
            sqbB = [Buf(), Buf()]
            rstd = carve([TB], F32)
            rstdB = Buf()
            FB = 512
            sg = [carve([FB], F32) for _ in range(2)]
            sgB = [Buf(), Buf()]
            aT = carve([2, 2, FB], BF16)
            aTB = [[Buf(), Buf()], [Buf(), Buf()]]
            hn = win
            blocks = [(b, b * TB, TB) for b in range(npb)]
            if seg == 0:
                blocks.append((npb, TP, 128))

            def load_slice(j):
                bf = j % 2
                D(dma(wgs[bf][:], d_wg[l].rearrange("(k p) f -> p k f", p=128)[:, :, j * FS:(j + 1) * FS], "pool"),
                  [], [wsB[bf][0]], q="pool")
                D(dma(wus[bf][:], d_wu[l].rearrange("(k p) f -> p k f", p=128)[:, :, j * FS:(j + 1) * FS], "pool"),
                  [], [wsB[bf][1]], q="pool")
                D(dma(wds[bf][:], d_wd[l, j * FS:(j + 1) * FS, :].rearrange("(t p) o -> p t o", p=128), "pool"),
                  [], [wsB[bf][2]], q="pool")

            load_slice(0)
            for (blk, c0, tb) in blocks:
                rmsnorm_block(l, 8, blk, c0, tb, lambda k: hn[:, k, c0:c0 + tb], lambda k: winB[k], sqb, sqbB, rstd, rstdB)
            fblocks = []
            g = FB // TB
            for b0 in range(0, npb, g):
                ids = list(range(b0, min(npb, b0 + g)))
                fblocks.append((ids, b0 * TB, len(ids) * TB))
            if seg == 0:
                fblocks.append(([npb], TP, 128))
            it = 0
            for j in range(nsl):
                if j + 1 < nsl:
                    load_slice(j + 1)
                bf = j % 2
                for (ids, c0, tb) in fblocks:
                    ab = it % 2
                    it += 1
                    for dt_ in range(2):
                        pg, pgB = bank()
                        for k in range(8):
                            T(mm(pg[:, 0:tb], wgs[bf][:, k, dt_ * 128:(dt_ + 1) * 128], hn[:, k, c0:c0 + tb], k == 0, k == 7),
                              [wsB[bf][0], winB[k]], [pgB])
                        pu, puB = bank()
                        for k in range(8):
                            T(mm(pu[:, 0:tb], wus[bf][:, k, dt_ * 128:(dt_ + 1) * 128], hn[:, k, c0:c0 + tb], k == 0, k == 7),
                              [wsB[bf][1], winB[k]], [puB])
                        A(act(sg[dt_][:, 0:tb], pg[:, 0:tb], AF.Silu), [pgB], [sgB[dt_]])
                        V(tt(aT[:, ab, dt_, 0:tb], sg[dt_][:, 0:tb], pu[:, 0:tb], ALU.mult), [sgB[dt_], puB], [aTB[ab][dt_]])
                    for ot in range(8):
                        po, poB = bank()
                        for dt_ in range(2):
                            T(mm(po[:, 0:tb], wds[bf][:, dt_, ot * 128:(ot + 1) * 128], aT[:, ab, dt_, 0:tb], dt_ == 0, dt_ == 1),
                              [wsB[bf][2], aTB[ab][dt_]], [poB])
                        xb_ = [xB[ot][i_] for i_ in ids]
                        V(tt(xT[:, ot, c0:c0 + tb], xT[:, ot, c0:c0 + tb], po[:, 0:tb], ALU.add), xb_ + [poB], xb_)
                        if j == nsl - 1 and l == NL - 1:
                            if ids[0] == npb:
                                D(dma(o_ys[ot * 128:(ot + 1) * 128, :], xT[:, ot, TP:TP + 128]), xb_, [])
                            else:
                                D(dma(o_yp[seg, ot * 128:(ot + 1) * 128, c0:c0 + tb], xT[:, ot, c0:c0 + tb]), xb_, [])

        for seg in range(n_seg):
            load_segment(seg)
            for l in range(NL):
                S.epoch()
                mixer_phase(seg, l)
                S.epoch()
                ffn_phase(seg, l)
            S.mute = False
            store_segment(seg)
            S.barrier()
        S.emit()
        build.stats = S.stats
    return nc


def _consts():
    cf = np.zeros((128, 8, 128), np.float32)
    cf[:, 0, :] = np.eye(128, dtype=np.float32)
    cf[:, 1, :] = 1.0 / 1024.0
    cf[:, 2, :] = 1.0 / 256.0
    blk = np.zeros((128, 128), np.float32)
    blk[0:64, 0:64] = 1.0
    blk[64:128, 64:128] = 1.0
    cf[:, 3, :] = blk / 64.0
    cf[:, 4, :] = blk
    cf[:, 5, :] = 1.0
    cf[0:64, 6, 64:128] = np.eye(64, dtype=np.float32)
    for p in range(128):
        cf[p, 7, p % 64] = 1.0
    i = np.arange(64)[:, None]
    t = np.arange(64)[None, :]
    m1 = np.ones((64, 512), np.float32)
    m1[:, 0:64] = (i < t)
    m1[:, 64:128] = (i <= t)
    m1[:, 192:256] = (i <= t)
    m1[:, 320:384] = (i > t)
    m1[:, 384:448] = (i > t)
    m2 = np.zeros((64, 128), np.float32)
    m2[:, 0:64] = (i > t)
    m2[:, 64:128] = (i > t)
    scanm = np.ones((128, TB), np.float32)
    scanm[:, ::64] = 0.0
    icnt = np.zeros((128, 2, TB), np.float32)
    wins = {(0, 0): 2, (0, 1): 4, (1, 0): 8, (1, 1): 16}
    pos = np.arange(TB)
    for ct in range(2):
        for hf in range(2):
            w = wins[(ct, hf)]
            icnt[hf * 64:(hf + 1) * 64, ct, :] = 1.0 / np.minimum(w, pos + 1).astype(np.float32)[None, :]
    m1 = np.concatenate([m1, m1], axis=0)
    cf[64:128, 6, 0:64] = np.eye(64, dtype=np.float32)
    return cf, m1, m2, scanm, icnt


def _col(v):
    v = np.asarray(v, np.float32)
    return v.reshape(-1, 128).T


def prepare_shared(inp):
    f = lambda k: np.asarray(inp[k], np.float32)
    pvs = np.zeros((2, 128, NPV), np.float32)
    for l in range(2):
        p = pvs[l]
        p[:, 0:8] = _col(f("norm_mix_g")[l])
        p[:, 8:16] = _col(f("norm_ffn_g")[l])
        cw = f("conv_w")[l]
        for ct in range(2):
            p[:, 16 + ct * 31:16 + (ct + 1) * 31] = cw[:, ct * 128:(ct + 1) * 128].T
        p[:, 78:80] = _col(f("conv_b")[l])
        p[:, 80:82] = _col(f("conv_ln_g")[l])
        p[:, 82:84] = _col(f("conv_ln_b")[l])
        p[:, 84:91] = _col(f("rwkv_mu")[l])
        p[:, 91:93] = _col(f("rwkv_w0")[l])
        p[:, 93:95] = _col(f("rwkv_a0")[l])
        p[:, 95:97] = _col(f("rwkv_k_k")[l])
        p[:, 97:99] = _col(f("rwkv_k_a")[l])
        p[:, 99:101] = _col(f("rwkv_r_k")[l].reshape(-1))
        p[:, 101:103] = _col(f("rwkv_gn_g")[l])
        p[:, 103:105] = _col(f("rwkv_gn_b")[l])
        p[:, 105:107] = _col(f("pool_scale")[l])
        p[:, 107] = np.tile(f("attn_q_norm")[l], 2)
        p[:, 108] = np.tile(f("attn_k_norm")[l], 2)
        sk = f("attn_sinks")[l]
        for h in range(2):
            p[0:64, 109 + h] = sk[2 * h]
            p[64:128, 109 + h] = sk[2 * h + 1]
    w_in = f("w_in")
    base = 512 + 896 + 256
    qcols = lambda h: list(range(base + 64 * h, base + 64 * (h + 1)))
    perm = list(range(base)) + qcols(0) + qcols(2) + qcols(1) + qcols(3) + list(range(base + 256, 2176))
    w_in_p = np.ascontiguousarray(w_in[:, :, perm])
    lw = np.concatenate([f("rwkv_w2"), f("rwkv_a2"), f("rwkv_g2")], axis=1)
    pw = f("pool_w")
    poolw = np.zeros((2, 2, 128, 128), np.float32)
    for l in range(2):
        for g in range(4):
            ct, hf = g // 2, g % 2
            poolw[l, ct, hf * 64:(hf + 1) * 64, hf * 64:(hf + 1) * 64] = pw[l, g]
    cf, m1, m2, scanm, icnt = _consts()
    return dict(pv=pvs, w_in=w_in_p, w_out=f("w_out"), wg=f("ffn_w_gate"), wu=f("ffn_w_up"), wd=f("ffn_w_down"),
                lw=np.ascontiguousarray(lw), poolw=poolw, cf=cf, m1=m1, m2=m2, scanm=scanm, icnt=icnt)


def prepare_core(inp, c, n_seg, npb):
    f = lambda k: np.asarray(inp[k], np.float32)
    TP = npb * TB
    b = c % 4
    xp = f("x_prompt")[b, :n_seg * TP]
    xTp = np.ascontiguousarray(xp.reshape(n_seg, TP, 1024).transpose(0, 2, 1))
    xs = f("x_sample")[2 * c:2 * c + 2]
    xTs = np.zeros((1024, 128), np.float32)
    for s in range(2):
        xTs[:, 64 * s:64 * s + 32] = xs[s].T
    sl = slice(2 * c, 2 * c + 2)
    cconv = f("cache_conv")[:, sl]
    cconv = np.ascontiguousarray(cconv.reshape(2, 2, 30, 2, 128).transpose(0, 1, 4, 3, 2))
    srw = np.ascontiguousarray(f("state_rwkv")[:, sl].transpose(0, 1, 2, 4, 3))
    ssh = f("state_rwkv_shift")[:, sl]
    ssh = np.ascontiguousarray(ssh.reshape(2, 2, 7, 128).transpose(0, 1, 3, 2))
    cpool = f("cache_pool")[:, sl]
    cpool = np.ascontiguousarray(cpool.reshape(2, 2, 15, 2, 128).transpose(0, 1, 4, 3, 2))
    ck = f("cache_k")[:, sl].reshape(2, 2, 128, 128)
    ckT = np.ascontiguousarray(ck.transpose(0, 1, 3, 2))
    cv = np.ascontiguousarray(f("cache_v")[:, sl].reshape(2, 2, 128, 128))
    return dict(xTp=xTp, xTs=xTs, cconv=cconv, srw=srw, sshift=ssh, cpool=cpool, ckT=ckT, cv=cv)


_NC_CACHE = {}


def run(inp, n_seg, npb, n_cores=8):
    key = (n_seg, npb)
    if key not in _NC_CACHE:
        _NC_CACHE[key] = build(n_seg, npb)
    nc = _NC_CACHE[key]
    shared = prepare_shared(inp)
    in_maps = []
    for c in range(n_cores):
        m = dict(shared)
        m.update(prepare_core(inp, c, n_seg, npb))
        in_maps.append(m)
    res = run_bass_kernel_spmd(nc, in_maps, core_ids=list(range(n_cores)))
    R = res.results
    TP = npb * TB
    T = n_seg * TP
    nb = min(4, n_cores)
    y_p = np.zeros((4, T, 1024), np.float32)
    for b in range(nb):
        y_p[b] = R[b]["yTp"].transpose(0, 2, 1).reshape(T, 1024)
    ns = 2 * n_cores
    y_s = np.zeros((16, 32, 1024), np.float32)
    conv_p = np.zeros((2, 4, 30, 256), np.float32)
    conv_s = np.zeros((2, 16, 30, 256), np.float32)
    rw_p = np.zeros((2, 4, 4, 64, 64), np.float32)
    rw_s = np.zeros((2, 16, 4, 64, 64), np.float32)
    sh_p = np.zeros((2, 4, 896), np.float32)
    sh_s = np.zeros((2, 16, 896), np.float32)
    pool_p = np.zeros((2, 4, 15, 256), np.float32)
    pool_s = np.zeros((2, 16, 15, 256), np.float32)
    k_p = np.zeros((2, 4, 128, 2, 64), np.float32)
    k_s = np.zeros((2, 16, 128, 2, 64), np.float32)
    v_p = np.zeros((2, 4, 128, 2, 64), np.float32)
    v_s = np.zeros((2, 16, 128, 2, 64), np.float32)

    def put(dst_p, dst_s, name, conv):
        for c in range(n_cores):
            o = R[c][name]
            if c < nb:
                dst_p[:, c] = conv(o[:, 0])
            for s in range(2):
                dst_s[:, 2 * c + s] = conv(o[:, 1 + s])

    for c in range(n_cores):
        ys = R[c]["yTs"]
        for s in range(2):
            y_s[2 * c + s] = ys[:, 64 * s:64 * s + 32].T
    put(conv_p, conv_s, "o_conv", lambda o: o.transpose(0, 3, 2, 1).reshape(2, 30, 256))
    put(rw_p, rw_s, "o_rw", lambda o: o.transpose(0, 1, 3, 2))
    put(sh_p, sh_s, "o_shift", lambda o: o.transpose(0, 2, 1).reshape(2, 896))
    put(pool_p, pool_s, "o_pool", lambda o: o.transpose(0, 3, 2, 1).reshape(2, 15, 256))
    put(k_p, k_s, "o_k", lambda o: o.transpose(0, 2, 1).reshape(2, 128, 2, 64))
    put(v_p, v_s, "o_v", lambda o: o.reshape(2, 128, 2, 64))
    return (y_p, y_s, conv_p, conv_s, rw_p, rw_s, sh_p, sh_s, pool_p, pool_s, k_p, k_s, v_p, v_s)


def kernel(**inputs):
    return run(inputs, 2, 2048 // TB, 8)
```

```python
import contextlib
import os
DBG = os.environ.get('KDBG', 'ACDBOFR')
KB = int(os.environ.get('KB', '9'))
STRICT_ENGS = os.environ.get('KSE', '').split(',')
STRICT = os.environ.get('KSTRICT', '0') == '1'
import numpy as np
import concourse.bass as bass
import concourse.mybir as mybir
from concourse.bass_utils import run_bass_kernel_spmd

F32 = mybir.dt.float32
BF16 = mybir.dt.bfloat16
CH = BF16
ALU = mybir.AluOpType
AF = mybir.ActivationFunctionType

TB = 128
NPV = 111
RMS_EPS = 1e-6
LN_EPS = 1e-5
GN_EPS = 64e-5
DECAY_C = -0.6065306597126334


class Buf:
    __slots__ = ("name", "last_w", "readers")

    def __init__(self, name=""):
        self.name = name
        self.last_w = None
        self.readers = []


class Op:
    __slots__ = ("eng", "fn", "deps", "signal", "sigval", "is_dma", "dsem", "dval", "prev_same_sem", "epoch")

    def __init__(self, eng, fn, is_dma):
        self.eng = eng
        self.fn = fn
        self.deps = []
        self.signal = False
        self.sigval = 0
        self.is_dma = is_dma
        self.dsem = None
        self.dval = 0
        self.prev_same_sem = None
        self.epoch = 0


class Sched:
    ENGS = ("pe", "dve", "act", "pool", "sp")
    NDSEM = 12
    NQ = {"sp": 12, "pool": 6}

    def __init__(self, nc, stack):
        self.nc = nc
        self.stack = stack
        self.ops = []
        self.eng_obj = {"pe": nc.tensor, "dve": nc.vector, "act": nc.scalar, "pool": nc.gpsimd, "sp": nc.sync}
        self.last_op = {}
        self.dma_since = []
        self.bar_deps = {}
        self.mute = False
        self.cur_epoch = 0
        self.capture = None

    def op(self, eng, fn, reads=(), writes=(), dma=False, strict=False):
        if self.mute:
            return None
        if self.capture is not None:
            self.capture.append((eng, fn, reads, writes, dma, strict))
            return None
        o = Op(eng, fn, dma)
        o.epoch = self.cur_epoch
        deps = set()
        for b in reads:
            w = b.last_w
            if w is not None:
                deps.add(w)
        for b in writes:
            w = b.last_w
            if w is not None and (STRICT or strict or eng in STRICT_ENGS or w.is_dma or dma or w.eng != eng):
                deps.add(w)
            for r in b.readers:
                if STRICT or eng in STRICT_ENGS or r.is_dma or dma or r.eng != eng:
                    deps.add(r)
        if eng in self.bar_deps:
            for d in self.bar_deps.pop(eng):
                deps.add(d)
        o.deps = list(deps)
        for d in o.deps:
            d.signal = True
        for b in reads:
            b.readers.append(o)
        for b in writes:
            b.last_w = o
            b.readers = []
        if dma:
            o.signal = True
            self.dma_since.append(o)
        else:
            self.last_op[eng] = o
        self.ops.append(o)
        return o

    def barrier(self):
        deps = list(self.last_op.values()) + list(self.dma_since)
        self.dma_since = []
        for e in self.ENGS:
            self.bar_deps[e] = list(deps) + self.bar_deps.get(e, [])

    def epoch(self):
        self.ops.append(None)
        self.cur_epoch += 1
        self.last_op = {}
        self.dma_since = []
        self.bar_deps = {}

    def emit(self, final_wait_eng="sp"):
        nc = self.nc
        epochs = [[]]
        for o in self.ops:
            if o is None:
                epochs.append([])
            else:
                epochs[-1].append(o)
        esems = [{e: self.stack.enter_context(nc.semaphore("s%d_%s" % (ei, e))) for e in self.ENGS}
                 for ei in range(len(epochs))]
        dsems = {}
        for q in ("sp", "pool"):
            dsems[q] = [self.stack.enter_context(nc.semaphore("d_%s%d" % (q, i))) for i in range(self.NQ[q])]
        allsems = [x for es in esems for x in es.values()] + [x for q in dsems for x in dsems[q]]

        def clear_all():
            nc.all_engine_barrier()
            for s_ in allsems:
                nc.gpsimd.sem_clear(s_)
            nc.all_engine_barrier()

        nwaits = 0
        maxcnt = 0
        clear_all()
        dcount = {q: [0] * self.NQ[q] for q in dsems}
        dlast = {q: [None] * self.NQ[q] for q in dsems}
        drr = {q: 0 for q in dsems}
        seen_d = {e: {} for e in self.ENGS}
        for ei, ops in enumerate(epochs):
            sem = esems[ei]
            cnt = {e: 0 for e in self.ENGS}
            for o in ops:
                if o.is_dma:
                    q = o.eng
                    i = drr[q]
                    drr[q] = (i + 1) % self.NQ[q]
                    dcount[q][i] += 16
                    o.dsem = dsems[q][i]
                    o.dval = dcount[q][i]
                    o.prev_same_sem = dlast[q][i]
                    dlast[q][i] = o
                elif o.signal:
                    cnt[o.eng] += 1
                    o.sigval = cnt[o.eng]
            maxcnt = max(maxcnt, max(cnt.values()))
            seen = {e: {} for e in self.ENGS}
            for o in ops:
                e = o.eng
                eo = self.eng_obj[e]
                deps = [d for d in o.deps if d.epoch == ei]
                if o.is_dma and o.prev_same_sem is not None:
                    deps.append(o.prev_same_sem)
                for d in deps:
                    if d.is_dma:
                        if seen_d[e].get(id(d.dsem), 0) < d.dval:
                            eo.wait_ge(d.dsem, d.dval)
                            seen_d[e][id(d.dsem)] = d.dval
                            nwaits += 1
                    else:
                        if seen[e].get(d.eng, 0) < d.sigval:
                            eo.wait_ge(sem[d.eng], d.sigval)
                            seen[e][d.eng] = d.sigval
                            nwaits += 1
                inst = o.fn()
                if o.is_dma:
                    inst.then_inc(o.dsem, 16)
                elif o.signal:
                    inst.then_inc(sem[e], 1)
            for q in dsems:
                eo = self.eng_obj[q]
                for i in range(self.NQ[q]):
                    d = dlast[q][i]
                    if d is not None and seen_d[q].get(id(d.dsem), 0) < d.dval:
                        eo.wait_ge(d.dsem, d.dval)
                        seen_d[q][id(d.dsem)] = d.dval
            if ei + 1 < len(epochs):
                nc.all_engine_barrier()
        clear_all()
        self.stats = dict(n_ops=len(self.ops), n_waits=nwaits, maxcnt=maxcnt, n_epochs=len(epochs))


def build(n_seg, npb, n_layers=2):
    nc = bass.Bass("TRN2", target_bir_lowering=False)
    TP = npb * TB
    NT = TP + 128
    NL = n_layers

    def din(name, shape):
        return nc.dram_tensor(name, list(shape), F32, kind="ExternalInput").ap()

    def dout(name, shape):
        return nc.dram_tensor(name, list(shape), F32, kind="ExternalOutput").ap()

    d_xp = din("xTp", [n_seg, 1024, TP])
    d_xs = din("xTs", [1024, 128])
    d_pv = din("pv", [2, 128, NPV])
    d_win = din("w_in", [2, 1024, 2176])
    d_wout = din("w_out", [2, 1024, 1024])
    d_wg = din("wg", [2, 1024, 2816])
    d_wu = din("wu", [2, 1024, 2816])
    d_wd = din("wd", [2, 2816, 1024])
    d_lw = din("lw", [2, 128, 256])
    d_poolw = din("poolw", [2, 2, 128, 128])
    d_cf = din("cf", [128, 8, 128])
    d_m1 = din("m1", [128, 512])
    d_m2 = din("m2", [64, 128])
    d_scan = din("scanm", [128, TB])
    d_icnt = din("icnt", [128, 2, TB])
    d_cconv = din("cconv", [2, 2, 128, 2, 30])
    d_srw = din("srw", [2, 2, 4, 64, 64])
    d_sshift = din("sshift", [2, 2, 128, 7])
    d_cpool = din("cpool", [2, 2, 128, 2, 15])
    d_ckT = din("ckT", [2, 2, 128, 128])
    d_cv = din("cv", [2, 2, 128, 128])

    o_yp = dout("yTp", [n_seg, 1024, TP])
    o_ys = dout("yTs", [1024, 128])
    o_conv = dout("o_conv", [2, 3, 128, 2, 30])
    o_rw = dout("o_rw", [2, 3, 4, 64, 64])
    o_shift = dout("o_shift", [2, 3, 128, 7])
    o_pool = dout("o_pool", [2, 3, 128, 2, 15])
    o_k = dout("o_k", [2, 3, 128, 128])
    o_v = dout("o_v", [2, 3, 128, 128])

    st = contextlib.ExitStack()
    with st:
        S = Sched(nc, st)

        def sb(name, shape, dt):
            return st.enter_context(nc.sbuf_tensor(name, list(shape), dt))

        xT = sb("xT", [128, 8, NT], F32)
        xB = [[Buf() for _ in range(npb + 1)] for _ in range(8)]
        win = sb("win", [128, 8, 2176], BF16)
        winB = [Buf() for _ in range(8)]
        wout = sb("wout", [128, 8, 1024], BF16)
        woutB = Buf()
        pv = sb("pvt", [128, 2, NPV], F32)
        pvB = Buf()
        dv = sb("dvt", [128, 2, 16], F32)
        dvB = Buf()
        cf = sb("cft", [128, 8, 128], F32)
        cfB = Buf()
        cb16 = sb("cb16", [128, 7, 128], BF16)
        cb16B = Buf()
        m1 = sb("m1t", [128, 512], F32)
        mX = m1
        m2 = sb("m2t", [64, 128], F32)
        scanm = sb("scanmt", [128, TB], F32)
        icnt = sb("icntt", [128, 2, TB], F32)
        cmB = Buf()
        lw = sb("lwt", [128, 256], BF16)
        lwB = Buf()
        poolw = sb("poolwt", [128, 2, 128], BF16)
        poolwB = Buf()
        cbP = [sb("cbP%d" % l, [128, 2, 30 + TB], F32) for l in range(2)]
        pbP = [sb("pbP%d" % l, [128, 2, 16 + TB], F32) for l in range(2)]
        kbP = [sb("kbP%d" % l, [128, 128 + TB], BF16) for l in range(2)]
        vbP = [sb("vbP%d" % l, [64, 2 + TB // 64, 2, 128], BF16) for l in range(2)]
        shP = [sb("shP%d" % l, [128, 7], F32) for l in range(2)]
        SsP = [sb("SsP%d" % l, [128, 4, 128], F32) for l in range(2)]
        cbPB = [Buf() for _ in range(2)]
        pbPB = [Buf() for _ in range(2)]
        kbPB = [Buf() for _ in range(2)]
        vbPB = [Buf() for _ in range(2)]
        shPB = [Buf() for _ in range(2)]
        SsPB = [[Buf() for _ in range(4)] for _ in range(2)]
        cbS = [sb("cbS%d" % s, [128, 2, 30 + 64], F32) for s in range(2)]
        pbS = [sb("pbS%d" % s, [128, 2, 16 + 64], F32) for s in range(2)]
        kbS = [sb("kbS%d" % s, [128, 128 + 64], BF16) for s in range(2)]
        vbS = [sb("vbS%d" % s, [64, 3, 2, 128], BF16) for s in range(2)]
        shS = [sb("shS%d" % s, [128, 7], F32) for s in range(2)]
        SsS = [sb("SsS%d" % s, [128, 4, 128], F32) for s in range(2)]
        cbSB = [Buf() for _ in range(2)]
        pbSB = [Buf() for _ in range(2)]
        kbSB = [Buf() for _ in range(2)]
        vbSB = [Buf() for _ in range(2)]
        shSB = [Buf() for _ in range(2)]
        SsSB = [[Buf() for _ in range(4)] for _ in range(2)]
        bnk = sb("bnk", [128, 2, 96], F32)
        bnv = sb("bnv", [128, 2, 128], F32)
        bnkB = [Buf(), Buf()]
        bnvB = [Buf(), Buf()]

        ARENA_F32 = 12416
        arena = sb("arena", [128, ARENA_F32], F32)
        apos = [0]

        def carve(shape, dt, parts=128):
            n = 1
            for s_ in shape:
                n *= s_
            nb = n * (4 if dt == F32 else 2)
            nf = (nb + 3) // 4
            a0 = apos[0]
            apos[0] += nf
            build.amax = max(getattr(build, 'amax', 0), apos[0])
            assert apos[0] <= ARENA_F32, ("arena overflow", apos[0])
            v = arena[0:parts, a0:a0 + nf]
            if dt != F32:
                v = v.bitcast(dt)
            if len(shape) == 1:
                return v
            names = " ".join("d%d" % i for i in range(len(shape)))
            kw = {"d%d" % i: shape[i] for i in range(len(shape))}
            return v.rearrange("p (%s) -> p %s" % (names, names), **kw)

        banks = [st.enter_context(nc.psum_tensor("bank%d" % i, [128, 512], F32)) for i in range(8)]
        bankB = [Buf() for _ in range(8)]
        brr = {"all": 0, "w": 0, "a": 0, "f": 0}
        bpool = ["all"]
        BPOOLS = {"all": list(range(8)), "w": [0, 1, 2, 3], "a": [4, 5], "f": [6, 7]}

        def bank():
            p = bpool[0]
            lst = BPOOLS[p]
            i = lst[brr[p] % len(lst)]
            brr[p] += 1
            return banks[i], bankB[i]

        def V(fn, r, w):
            S.op("dve", fn, r, w)

        def A(fn, r, w):
            S.op("act", fn, r, w)

        def G(fn, r, w):
            S.op("pool", fn, r, w)

        def T(fn, r, w):
            f_, start_ = fn
            S.op("pe", f_, r, w, strict=start_)

        def D(fn, r, w, q="sp"):
            S.op(q, fn, r, w, dma=True)

        def mm(o, l, r, a=True, b=True, order=False):
            return (lambda: nc.tensor.matmul(o, lhsT=l, rhs=r, start=a, stop=b)), (a or order or l.dtype == F32)

        def tr(o, i, ident):
            return (lambda: nc.tensor.transpose(o, i, ident)), True

        def tt(o, a, b, op):
            return lambda: nc.vector.tensor_tensor(out=o, in0=a, in1=b, op=op)

        def ts(o, a, s1, op0, s2=None, op1=None):
            if op1 is None:
                return lambda: nc.vector.tensor_scalar(out=o, in0=a, scalar1=s1, scalar2=None, op0=op0)
            return lambda: nc.vector.tensor_scalar(out=o, in0=a, scalar1=s1, scalar2=s2, op0=op0, op1=op1)

        def stt(o, a, s, b, op0, op1):
            return lambda: nc.vector.scalar_tensor_tensor(out=o, in0=a, scalar=s, in1=b, op0=op0, op1=op1)

        def act(o, i, f, bias=None, scale=None):
            kw = {}
            if bias is not None:
                kw["bias"] = bias
            if scale is not None:
                kw["scale"] = scale
            return lambda: nc.scalar.activation(out=o, in_=i, func=f, **kw)

        def vcopy(o, i):
            return lambda: nc.vector.tensor_copy(out=o, in_=i)

        def gcopy(o, i):
            return lambda: nc.gpsimd.tensor_copy(out=o, in_=i)

        def gmemset(o, v):
            return lambda: nc.gpsimd.memset(o, v)

        def dma(o, i, q="sp"):
            if q == "sp":
                return lambda: nc.sync.dma_start(out=o, in_=i)
            return lambda: nc.gpsimd.dma_start(out=o, in_=i)

        def rsqrt_act(dst, src, eps, rd, wr, tmp=None):
            t_ = dst if tmp is None else tmp
            A(act(t_, src, AF.Ln, bias=eps), rd, wr)
            A(act(dst, t_, AF.Exp, scale=-0.5), wr, wr)

        def sigmoid_le(dst, src, tmp, rd, wr, tmpB, nbias=None, xscale=1.0):
            if nbias is None:
                A(act(tmp, src, AF.Exp, scale=-xscale), rd, [tmpB])
            else:
                A(act(tmp, src, AF.Exp, bias=nbias, scale=-xscale), rd, [tmpB])
            A(act(tmp, tmp, AF.Ln, bias=1.0), [tmpB], [tmpB])
            A(act(dst, tmp, AF.Exp, scale=-1.0), [tmpB], wr)

        D(dma(pv[:], d_pv.rearrange("l p n -> p l n")), [], [pvB])
        D(dma(cf[:], d_cf), [], [cfB])
        D(dma(m1[:], d_m1), [], [cmB])
        D(dma(m2[:], d_m2), [], [cmB])
        D(dma(scanm[:], d_scan), [], [cmB])
        D(dma(icnt[:], d_icnt), [], [cmB])
        IDN = cf[:, 0, :]
        ONES256 = cf[:, 2, :]
        BLKM = cf[:, 3, :]
        BLKS = cf[:, 4, :]
        ISH = cf[:, 6, :]
        IDUP = cf[:, 7, 0:64]
        G(gcopy(cb16[:, 0, :], cf[:, 1, :]), [cfB], [cb16B])
        G(gcopy(cb16[:, 1, :], cf[:, 3, :]), [cfB], [cb16B])
        G(gcopy(cb16[:, 2, :], cf[:, 5, :]), [cfB], [cb16B])
        G(gcopy(cb16[:, 3, :], cf[:, 0, :]), [cfB], [cb16B])
        G(gcopy(cb16[:, 4, :], cf[:, 6, :]), [cfB], [cb16B])
        G(gcopy(cb16[:, 5, :], cf[:, 4, :]), [cfB], [cb16B])
        G(gcopy(cb16[:, 6, :], cf[:, 2, :]), [cfB], [cb16B])
        BLKS_B = cb16[:, 5, :]
        ONES256_B = cb16[:, 6, :]
        IDN_B = cb16[:, 3, :]
        ISH_B = cb16[:, 4, :]
        ONES1024_B = cb16[:, 0, :]
        BLKM_B = cb16[:, 1, :]
        ONES_B = cb16[:, 2, :]
        for l in range(2):
            V(ts(dv[:, l, 0:7], pv[:, l, 84:91], -1.0, ALU.mult, 1.0, ALU.add), [pvB], [dvB])
            V(ts(dv[:, l, 7:9], pv[:, l, 97:99], -1.0, ALU.mult, 1.0, ALU.add), [pvB], [dvB])
            A(act(dv[:, l, 9:11], pv[:, l, 109:111], AF.Exp), [pvB], [dvB])
            V(ts(dv[:, l, 12:16], pv[:, l, 91:95], -1.0, ALU.mult), [pvB], [dvB])

        def PV(l, c, p0=0, p1=128):
            return pv[p0:p1, l, c:c + 1]

        def DVc(l, c, p0=0, p1=128):
            return dv[p0:p1, l, c:c + 1]

        for l in range(2):
            G(gmemset(cbP[l][:, :, 0:30], 0.0), [], [cbPB[l]])
            G(gmemset(pbP[l][:, :, 0:16], 0.0), [], [pbPB[l]])
            G(gmemset(kbP[l][:, 0:128], 0.0), [], [kbPB[l]])
            G(gmemset(vbP[l][:, 0:2], 0.0), [], [vbPB[l]])
            G(gmemset(shP[l][:], 0.0), [], [shPB[l]])
            G(gmemset(SsP[l][:], 0.0), [], SsPB[l])

        def rmsnorm_block(l, gcol0, blk, c0, tb, dst, dstB, sqb, sqbB, rstd, rstdB):
            ps, psB = bank()
            for k in range(8):
                A(act(sqb[:, k % 2, 0:tb], xT[:, k, c0:c0 + tb], AF.Square), [xB[k][blk]], [sqbB[k % 2]])
                T(mm(ps[:, 0:tb], ONES1024_B, sqb[:, k % 2, 0:tb], k == 0, k == 7), [cb16B, sqbB[k % 2]], [psB])
            rsqrt_act(rstd[:, 0:tb], ps[:, 0:tb], RMS_EPS, [psB], [rstdB])
            for k in range(8):
                V(stt(dst(k), xT[:, k, c0:c0 + tb], PV(l, gcol0 + k), rstd[:, 0:tb], ALU.mult, ALU.mult),
                  [xB[k][blk], pvB, rstdB], [dstB(k)])

        def load_segment(seg):
            for g0 in range(0, npb, 4):
                g1 = min(npb, g0 + 4)
                for k in range(8):
                    D(dma(xT[:, k, g0 * TB:g1 * TB], d_xp[seg, k * 128:(k + 1) * 128, g0 * TB:g1 * TB]), [],
                      [xB[k][b] for b in range(g0, g1)])
            if seg == 0:
                for k in range(8):
                    D(dma(xT[:, k, TP:TP + 128], d_xs[k * 128:(k + 1) * 128, :]), [], [xB[k][npb]])

        def store_segment(seg):
            return

        def mixer_phase(seg, l):
            apos[0] = 0
            S.mute = False
            last_seg = (seg == n_seg - 1)
            for k in range(8):
                D(dma(win[:, k, :], d_win[l, k * 128:(k + 1) * 128, :], "pool"), [], [winB[k]], q="pool")
            for k in range(8):
                D(dma(wout[:, k, :], d_wout[l, k * 128:(k + 1) * 128, :], "pool"), [], [woutB], q="pool")
            D(dma(lw[:], d_lw[l], "pool"), [], [lwB], q="pool")
            D(dma(poolw[:], d_poolw[l].rearrange("t p o -> p t o"), "pool"), [], [poolwB], q="pool")
            xnP = [carve([8, TB], BF16) for _ in range(2)]
            xnPB = [[Buf() for _ in range(8)] for _ in range(2)]
            sqb = carve([2, TB], BF16)
            sqbB = [Buf(), Buf()]
            mix = carve([8, TB], BF16)
            mixB = [Buf() for _ in range(8)]
            ub = carve([5, TB], F32)
            ubB = [Buf() for _ in range(5)]
            ubvP = [carve([2, TB], F32) for _ in range(2)]
            ubvPB = [[Buf(), Buf()] for _ in range(2)]
            NSCR = 9
            scrF = [carve([TB], F32) for _ in range(NSCR)]
            scrFB = [Buf() for _ in range(NSCR)]
            scrE = [None] + [carve([TB], F32) for _ in range(5)]
            scrEB = [None] + [Buf() for _ in range(5)]
            AR3cP = [[carve([TB // 64, 3, 64], CH) for _ in range(2)] for _ in range(2)]
            AR3cPB = [[Buf(), Buf()] for _ in range(2)]
            BKcP = [[carve([TB // 64, 2, 64], CH) for _ in range(2)] for _ in range(2)]
            BKcPB = [[Buf(), Buf()] for _ in range(2)]
            BhKhcP = [[carve([2, TB], CH) for _ in range(2)] for _ in range(2)]
            BhKhcPB = [[Buf(), Buf()] for _ in range(2)]
            yb = carve([2, TB], F32)
            ybB = [Buf(), Buf()]
            bonP = [carve([2, TB], F32) for _ in range(2)]
            bonPB = [[Buf(), Buf()] for _ in range(2)]
            gateP = [carve([2, TB], F32) for _ in range(2)]
            gatePB = [[Buf(), Buf()] for _ in range(2)]
            gamP = [carve([2, TB // 64], F32) for _ in range(2)]
            gamPB = [Buf(), Buf()]
            latb = carve([TB], BF16)
            latbB = Buf()
            qn = carve([2, TB], BF16)
            qnB = Buf()
            kf32 = carve([TB], F32)
            kf32B = Buf()
            vf32 = carve([TB // 64, 128], F32, parts=64)
            vf32B = Buf()
            Et = carve([3, 128], BF16, parts=64)
            EtB = Buf()
            rden = carve([128], F32)
            rdenB = Buf()
            dbf = carve([TB], BF16)
            psc = carve([4, 16 + TB], F32)
            pscB = [Buf() for _ in range(4)]
            dbfB = Buf()
            Xw = carve([4, 512], CH)
            XwB = Buf()
            XwhB = [Buf() for _ in range(4)]
            PmhB = [[Buf() for _ in range(4)] for _ in range(2)]
            N2hB = [[Buf() for _ in range(4)] for _ in range(2)]
            W3hB = [Buf() for _ in range(4)]
            F4hB = [Buf() for _ in range(4)]
            N2b = [carve([4, 128], CH) for _ in range(2)]
            N2bB = [Buf(), Buf()]
            Pm = [carve([4, 64], CH) for _ in range(2)]
            PmB = [Buf(), Buf()]
            W3 = carve([4, 128], CH)
            W3B = Buf()
            F4 = carve([2, 4, 128], CH)
            S16 = carve([4, 128], CH)
            S16B = Buf()
            F4B = Buf()
            vdup = carve([4, 128], CH, parts=64)
            vdupB = Buf()

            if seg == 0:
                for s in range(2):
                    D(dma(cbS[s][:, :, 0:30], d_cconv[l, s]), [], [cbSB[s]])
                    D(dma(pbS[s][:, :, 1:16], d_cpool[l, s]), [], [pbSB[s]])
                    D(dma(kbS[s][:, 0:128], d_ckT[l, s], "pool"), [], [kbSB[s]], q="pool")
                    for c in range(2):
                        for dd in range(2):
                            D(dma(vbS[s][:, c, :, dd * 64:(dd + 1) * 64],
                                  d_cv[l, s, c * 64:(c + 1) * 64, :].rearrange("t (h d) -> t h d", h=2), "pool"),
                              [], [vbSB[s]], q="pool")
                    D(dma(shS[s][:], d_sshift[l, s]), [], [shSB[s]])
                    for dd in range(2):
                        D(dma(SsS[s][64:128, :, dd * 64:(dd + 1) * 64], d_srw[l, s].rearrange("h k v -> k h v")),
                          [], SsSB[s])
                    D(dma(bnk[:, s, :], d_ckT[l, s, :, 32:128]), [], [bnkB[s]])
                    D(dma(o_k[l, 1 + s, :, 0:96], bnk[:, s, :]), [bnkB[s]], [])
                    D(dma(bnv[0:96, s, :], d_cv[l, s, 32:128, :]), [], [bnvB[s]])
                    D(dma(o_v[l, 1 + s, 0:96, :], bnv[0:96, s, :]), [bnvB[s]], [])

            blocks = []
            for b in range(npb):
                blocks.append(dict(blk=b, c0=b * TB, tb=TB, sample=False,
                                   pieces=[dict(pc0=0, P=TB, nreal=TB, seq=0, nch=TB // 64,
                                                g0=(seg * npb + b) * (TB // 64))]))
            if seg == 0:
                blocks.append(dict(blk=npb, c0=TP, tb=128, sample=True,
                                   pieces=[dict(pc0=0, P=64, nreal=32, seq=1, nch=1, g0=100),
                                           dict(pc0=64, P=64, nreal=32, seq=2, nch=1, g0=100)]))

            def FE(B, par):
                blk, c0, tb = B["blk"], B["c0"], B["tb"]
                is_last_blk = last_seg and (not B["sample"]) and blk == npb - 1
                xn, xnB = xnP[par], xnPB[par]
                AR3c, AR3cB, BKc, BKcB, BhKhc, BhKhcB = AR3cP[par], AR3cPB[par], BKcP[par], BKcPB[par], BhKhcP[par], BhKhcPB[par]
                gam, gamB, bon, bonB, gate, gateB = gamP[par], gamPB[par], bonP[par], bonPB[par], gateP[par], gatePB[par]
                ubv_, ubvB_ = ubvP[par], ubvPB[par]

                def ubt(ti):
                    if ti in (4, 5):
                        return ubv_[:, ti - 4, :]
                    return ub[:, 4, :] if ti == 6 else ub[:, ti, :]

                def ubtB(ti):
                    if ti in (4, 5):
                        return ubvB_[ti - 4]
                    return ubB[4] if ti == 6 else ubB[ti]
                scr, scrB = scrF, scrFB
                bpool[0] = 'f'
                rmsnorm_block(l, 0, blk, c0, tb, lambda k: xn[:, k, 0:tb], lambda k: xnB[k], sqb, sqbB, scr[0], scrB[0])

                def proj_tile(ti):
                    ps, psB = bank()
                    for k in range(8):
                        T(mm(ps[:, 0:tb], win[:, k, ti * 128:(ti + 1) * 128], xn[:, k, 0:tb], k == 0, k == 7),
                          [winB[k], xnB[k]], [psB])
                    return ps, psB

                def piece_bufs(pc):
                    if pc["seq"] == 0:
                        return cbP[l], cbPB[l], pbP[l], pbPB[l], kbP[l], kbPB[l], vbP[l], vbPB[l], shP[l], shPB[l], SsP[l], SsPB[l]
                    s = pc["seq"] - 1
                    return cbS[s], cbSB[s], pbS[s], pbSB[s], kbS[s], kbSB[s], vbS[s], vbSB[s], shS[s], shSB[s], SsS[s], SsSB[s]

                for ti in range(7):
                    pu, puB = proj_tile(4 + ti)
                    A(act(ubt(ti)[:, 0:tb], pu[:, 0:tb], AF.Copy), [puB], [ubtB(ti)])
                for pc in B["pieces"]:
                    sh, shB = piece_bufs(pc)[8:10]
                    a0, P, nreal = pc["pc0"], pc["P"], pc["nreal"]
                    for ti in range(7):
                        tmp = scr[1 + (ti % 2)]
                        tmpB = scrB[1 + (ti % 2)]
                        V(ts(tmp[:, 0:1], sh[:, ti:ti + 1], PV(l, 84 + ti), ALU.mult), [shB, pvB], [tmpB])
                        V(ts(tmp[:, 1:P], ubt(ti)[:, a0:a0 + P - 1], PV(l, 84 + ti), ALU.mult), [ubtB(ti), pvB], [tmpB])
                        G(gcopy(sh[:, ti:ti + 1], ubt(ti)[:, a0 + nreal - 1:a0 + nreal]), [ubtB(ti), tmpB], [shB])
                        V(stt(ubt(ti)[:, a0:a0 + P], ubt(ti)[:, a0:a0 + P], DVc(l, ti), tmp[:, 0:P], ALU.mult, ALU.add),
                          [ubtB(ti), dvB, tmpB, shB], [ubtB(ti)])
                        if nreal < P and ti < 6:
                            G(gmemset(ubt(ti)[:, a0 + nreal:a0 + P], 0.0), [ubtB(ti)], [ubtB(ti)])
                    if pc["seq"] == 0:
                        if is_last_blk:
                            D(dma(o_shift[l, 0], sh[:]), [shB], [])
                    else:
                        D(dma(o_shift[l, pc["seq"]], sh[:]), [shB], [])
                sigmoid_le(scr[3][0:32, 0:tb], ubt(6)[0:32, 0:tb], scr[3][0:32, 0:tb], [ubtB(6)], [scrB[3]], scrB[3], xscale=2.0)
                A(act(latb[0:32, 0:tb], scr[3][0:32, 0:tb], AF.Copy, bias=-1.0, scale=2.0), [scrB[3]], [latbB])
                A(act(latb[32:64, 0:tb], ubt(6)[32:64, 0:tb], AF.Copy), [ubtB(6)], [latbB])
                sigmoid_le(latb[64:128, 0:tb], ubt(6)[64:128, 0:tb], scr[3][64:128, 0:tb], [ubtB(6)], [latbB], scrB[3])
                nchb = tb // 64
                for ct in range(2):
                    r_, k_, v_ = ubt(ct)[:, 0:tb], ubt(2 + ct)[:, 0:tb], ubt(4 + ct)[:, 0:tb]
                    rB, kB, vB = ubtB(ct), ubtB(2 + ct), ubtB(4 + ct)
                    ld, ldB = scr[1][:, 0:tb], scrB[1]
                    Lc, LcB = scr[2][:, 0:tb], scrB[2]
                    ar, arB = scr[3][:, 0:tb], scrB[3]
                    kkn, kknB = scr[4][:, 0:tb], scrB[4]
                    kmod, kmodB = scr[5][:, 0:tb], scrB[5]
                    bvec, bvecB = scr[6][:, 0:tb], scrB[6]
                    t1, t1B = scr[7][:, 0:tb], scrB[7]
                    t2, t2B = scr[8][:, 0:tb], scrB[8]
                    pw, pwB = bank()
                    T(mm(pw[:, 0:tb], lw[0:32, ct * 128:(ct + 1) * 128], latb[0:32, 0:tb]), [lwB, latbB], [pwB])
                    sigmoid_le(ld, pw[:, 0:tb], ld, [pwB, dvB], [ldB], ldB, nbias=DVc(l, 12 + ct))
                    pa, paB = bank()
                    T(mm(pa[:, 0:tb], lw[32:64, ct * 128:(ct + 1) * 128], latb[32:64, 0:tb]), [lwB, latbB], [paB])
                    sigmoid_le(ar, pa[:, 0:tb], ar, [paB, dvB], [arB], arB, nbias=DVc(l, 14 + ct))
                    pg, pgB = bank()
                    T(mm(pg[:, 0:tb], lw[64:128, ct * 128:(ct + 1) * 128], latb[64:128, 0:tb]), [lwB, latbB], [pgB])
                    A(act(gate[:, ct, 0:tb], pg[:, 0:tb], AF.Copy), [pgB], [gateB[ct]])
                    V(ts(ld, ld, DECAY_C, ALU.mult), [ldB], [ldB])
                    if B["sample"]:
                        for pc in B["pieces"]:
                            G(gmemset(scr[1][:, pc["pc0"] + pc["nreal"]:pc["pc0"] + pc["P"]], 0.0), [ldB], [ldB])
                    V(lambda o=Lc, d0=scanm[:, 0:tb], d1=ld: nc.vector.tensor_tensor_scan(
                        out=o, data0=d0, data1=d1, initial=0.0, op0=ALU.mult, op1=ALU.add), [cmB, ldB], [LcB])
                    V(ts(t1, k_, PV(l, 95 + ct), ALU.mult), [kB, pvB], [t1B])
                    t2h = scr[8][:, 0:tb].bitcast(BF16)[:, 0:tb]
                    A(act(t2h, t1, AF.Square), [t1B], [t2B])
                    pss, pssB = bank()
                    T(mm(pss[:, 0:tb], BLKS_B, t2h), [cb16B, t2B], [pssB])
                    rsqrt_act(t2, pss[:, 0:tb], 1e-12, [pssB], [t2B])
                    V(tt(kkn, t1, t2, ALU.mult), [t1B, t2B], [kknB])
                    V(ts(t1, ar, PV(l, 97 + ct), ALU.mult, DVc(l, 7 + ct), ALU.add), [arB, pvB, dvB], [t1B])
                    V(tt(kmod, k_, t1, ALU.mult), [kB, t1B], [kmodB])
                    V(tt(bvec, kkn, ar, ALU.mult), [kknB, arB], [bvecB])
                    t1h = scr[7][:, 0:tb].bitcast(BF16)[:, 0:tb]
                    V(stt(t1h, r_, PV(l, 99 + ct), kmod, ALU.mult, ALU.mult), [rB, pvB, kmodB], [t1B])
                    pbn, pbnB = bank()
                    T(mm(pbn[:, 0:tb], BLKS_B, t1h), [cb16B, t1B], [pbnB])
                    V(tt(bon[:, ct, 0:tb], pbn[:, 0:tb], v_, ALU.mult), [pbnB, vB], [bonB[ct]])
                    A(act(t1, Lc, AF.Exp), [LcB], [t1B])
                    V(tt(AR3c[ct][:, 0:nchb, 1, :], r_.rearrange("p (c t) -> p c t", t=64), t1.rearrange("p (c t) -> p c t", t=64),
                         ALU.mult), [rB, t1B], [AR3cB[ct]])
                    A(act(gam[:, ct, 0:nchb], scr[2][:, 0:tb].rearrange("p (c t) -> p c t", t=64)[:, :, 63], AF.Exp),
                      [LcB], [gamB])
                    V(tt(t2, Lc, ld, ALU.subtract), [LcB, ldB], [t2B])
                    A(act(t2, t2, AF.Exp), [t2B], [t2B])
                    V(stt(AR3c[ct][:, 0:nchb, 0, :], kkn.rearrange("p (c t) -> p c t", t=64), -1.0,
                          t2.rearrange("p (c t) -> p c t", t=64), ALU.mult, ALU.mult), [kknB, t2B], [AR3cB[ct]])
                    A(act(t1, Lc, AF.Exp, scale=-1.0), [LcB], [t1B])
                    V(tt(BKc[ct][:, 0:nchb, 0, :], bvec.rearrange("p (c t) -> p c t", t=64), t1.rearrange("p (c t) -> p c t", t=64),
                         ALU.mult), [bvecB, t1B], [BKcB[ct]])
                    V(tt(BKc[ct][:, 0:nchb, 1, :], kmod.rearrange("p (c t) -> p c t", t=64), t1.rearrange("p (c t) -> p c t", t=64),
                         ALU.mult), [kmodB, t1B], [BKcB[ct]])
                    V(tt(t2.rearrange("p (c t) -> p c t", t=64),
                         scr[2][:, 0:tb].rearrange("p (c t) -> p c t", t=64)[:, :, 63:64].to_broadcast([128, nchb, 64]),
                         Lc.rearrange("p (c t) -> p c t", t=64), ALU.subtract), [LcB], [t2B])
                    A(act(t2, t2, AF.Exp), [t2B], [t2B])
                    V(tt(BhKhc[ct][:, 0, 0:tb], bvec, t2, ALU.mult), [bvecB, t2B], [BhKhcB[ct]])
                    V(tt(BhKhc[ct][:, 1, 0:tb], kmod, t2, ALU.mult), [kmodB, t2B], [BhKhcB[ct]])
                    for j in range(nchb):
                        V(ts(AR3c[ct][:, j, 2, :], IDUP, gam[:, ct, j:j + 1], ALU.mult), [cfB, gamB], [AR3cB[ct]])

            def BE(B, par):
                blk, c0, tb = B["blk"], B["c0"], B["tb"]
                is_last_blk = last_seg and (not B["sample"]) and blk == npb - 1
                xn, xnB = xnP[par], xnPB[par]
                AR3c, AR3cB, BKc, BKcB, BhKhc, BhKhcB = AR3cP[par], AR3cPB[par], BKcP[par], BKcPB[par], BhKhcP[par], BhKhcPB[par]
                gam, gamB, bon, bonB, gate, gateB = gamP[par], gamPB[par], bonP[par], bonPB[par], gateP[par], gatePB[par]
                ubv_, ubvB_ = ubvP[par], ubvPB[par]

                def ubt(ti):
                    if ti in (4, 5):
                        return ubv_[:, ti - 4, :]
                    return ub[:, 4, :] if ti == 6 else ub[:, ti, :]

                def ubtB(ti):
                    if ti in (4, 5):
                        return ubvB_[ti - 4]
                    return ubB[4] if ti == 6 else ubB[ti]
                scr, scrB = scrE, scrEB
                def proj_tile(ti):
                    ps, psB = bank()
                    for k in range(8):
                        T(mm(ps[:, 0:tb], win[:, k, ti * 128:(ti + 1) * 128], xn[:, k, 0:tb], k == 0, k == 7),
                          [winB[k], xnB[k]], [psB])
                    return ps, psB

                def piece_bufs(pc):
                    if pc["seq"] == 0:
                        return cbP[l], cbPB[l], pbP[l], pbPB[l], kbP[l], kbPB[l], vbP[l], vbPB[l], shP[l], shPB[l], SsP[l], SsPB[l]
                    s = pc["seq"] - 1
                    return cbS[s], cbSB[s], pbS[s], pbSB[s], kbS[s], kbSB[s], vbS[s], vbSB[s], shS[s], shSB[s], SsS[s], SsSB[s]

                outer_cap = S.capture
                S.capture = []
                bpool[0] = 'a'
                acc, accB = scr[1:3], scrB[1:3]
                cen, cenB = scr[3:5], scrB[3:5]
                for ct in range(2):
                    pval, pvalB = proj_tile(ct)
                    pgt, pgtB = proj_tile(2 + ct)
                    sigmoid_le(scr[5][:, 0:tb], pgt[:, 0:tb], scr[5][:, 0:tb], [pgtB], [scrB[5]], scrB[5])
                    for pc in B["pieces"]:
                        cb, cbB = piece_bufs(pc)[0:2]
                        a0, P = pc["pc0"], pc["P"]
                        V(tt(cb[:, ct, 30:30 + P], pval[:, a0:a0 + P], scr[5][:, a0:a0 + P], ALU.mult),
                          [pvalB, scrB[5]], [cbB])
                        V(ts(acc[ct][:, a0:a0 + P], cb[:, ct, 0:P], PV(l, 16 + ct * 31), ALU.mult, PV(l, 78 + ct), ALU.add),
                          [cbB, pvB], [accB[ct]])
                        for j in range(1, 31):
                            V(stt(acc[ct][:, a0:a0 + P], cb[:, ct, j:j + P], PV(l, 16 + ct * 31 + j), acc[ct][:, a0:a0 + P],
                                  ALU.mult, ALU.add), [cbB, pvB, accB[ct]], [accB[ct]])
                pm, pmB = bank()
                for ct in range(2):
                    T(mm(pm[:, 0:tb], ONES256, acc[ct][:, 0:tb], ct == 0, ct == 1), [cfB, accB[ct]], [pmB])
                for ct in range(2):
                    V(tt(cen[ct][:, 0:tb], acc[ct][:, 0:tb], pm[:, 0:tb], ALU.subtract), [accB[ct], pmB], [cenB[ct]])
                    A(act(acc[ct][:, 0:tb].bitcast(BF16)[:, 0:tb], cen[ct][:, 0:tb], AF.Square), [cenB[ct]], [accB[ct]])
                pvv, pvvB = bank()
                for ct in range(2):
                    T(mm(pvv[:, 0:tb], ONES256_B, acc[ct][:, 0:tb].bitcast(BF16)[:, 0:tb], ct == 0, ct == 1), [cb16B, accB[ct]], [pvvB])
                rsqrt_act(scr[5][:, 0:tb], pvv[:, 0:tb], LN_EPS, [pvvB], [scrB[5]])
                for ct in range(2):
                    V(tt(cen[ct][:, 0:tb], cen[ct][:, 0:tb], scr[5][:, 0:tb], ALU.mult), [cenB[ct], scrB[5]], [cenB[ct]])
                for ct in range(2):
                    A(act(acc[ct][:, 0:tb], cen[ct][:, 0:tb], AF.Identity, bias=PV(l, 82 + ct), scale=PV(l, 80 + ct)),
                      [cenB[ct], pvB], [accB[ct]])
                    sigmoid_le(scr[5][:, 0:tb], acc[ct][:, 0:tb], scr[5][:, 0:tb], [accB[ct]], [scrB[5]], scrB[5])
                    V(tt(mix[:, ct, 0:tb], acc[ct][:, 0:tb], scr[5][:, 0:tb], ALU.mult), [accB[ct], scrB[5]], [mixB[ct]])
                for pc in B["pieces"]:
                    cb, cbB = piece_bufs(pc)[0:2]
                    if pc["seq"] == 0:
                        G(gcopy(cb[:, :, 0:30], cb[:, :, TB:TB + 30]), [cbB], [cbB])
                        if is_last_blk:
                            D(dma(o_conv[l, 0], cb[:, :, 0:30]), [cbB], [])
                    else:
                        D(dma(o_conv[l, pc["seq"]], cb[:, :, 32:62]), [cbB], [])
                for ct in range(2):
                    pu, puB = proj_tile(11 + ct)
                    for pc in B["pieces"]:
                        pb, pbB = piece_bufs(pc)[2:4]
                        a0, P = pc["pc0"], pc["P"]
                        E_ = 16 + P
                        A(act(pb[:, ct, 16:16 + P], pu[:, a0:a0 + P], AF.Copy), [puB], [pbB])
                        ext = pb[:, ct, :]
                        s2, s4, s8, s16 = psc[:, 0, :], psc[:, 1, :], psc[:, 2, :], psc[:, 3, :]
                        V(tt(s2[:, 2:E_], ext[:, 2:E_], ext[:, 1:E_ - 1], ALU.add), [pbB], [pscB[0]])
                        V(tt(s4[:, 4:E_], s2[:, 4:E_], s2[:, 2:E_ - 2], ALU.add), [pscB[0]], [pscB[1]])
                        if ct == 1:
                            V(tt(s8[:, 8:E_], s4[:, 8:E_], s4[:, 4:E_ - 4], ALU.add), [pscB[1]], [pscB[2]])
                            V(tt(s16[:, 16:E_], s8[:, 16:E_], s8[:, 8:E_ - 8], ALU.add), [pscB[2]], [pscB[3]])
                            srcs = [(s8, pscB[2], 0.125), (s16, pscB[3], 0.0625)]
                        else:
                            srcs = [(s2, pscB[0], 0.5), (s4, pscB[1], 0.25)]
                        first = (pc["seq"] == 0 and seg == 0 and blk == 0)
                        for hf in range(2):
                            p0, p1 = hf * 64, hf * 64 + 64
                            sw, swB, iw = srcs[hf]
                            if first:
                                V(tt(scr[5][p0:p1, 0:P], sw[p0:p1, 16:E_], icnt[p0:p1, ct, 0:P], ALU.mult),
                                  [swB, cmB], [scrB[5]])
                                V(tt(dbf[p0:p1, a0:a0 + P], scr[5][p0:p1, 0:P], ext[p0:p1, 16:E_], ALU.subtract),
                                  [scrB[5], pbB], [dbfB])
                            else:
                                V(stt(dbf[p0:p1, a0:a0 + P], sw[p0:p1, 16:E_], iw, ext[p0:p1, 16:E_], ALU.mult, ALU.subtract),
                                  [swB, pbB], [dbfB])
                    py, pyB = bank()
                    T(mm(py[:, 0:tb], poolw[:, ct, :], dbf[:, 0:tb]), [poolwB, dbfB], [pyB])
                    A(act(mix[:, 4 + ct, 0:tb], py[:, 0:tb], AF.Copy, scale=PV(l, 105 + ct)), [pyB, pvB], [mixB[4 + ct]])
                for pc in B["pieces"]:
                    pb, pbB = piece_bufs(pc)[2:4]
                    if pc["seq"] == 0:
                        G(gcopy(pb[:, :, 1:16], pb[:, :, TB + 1:TB + 16]), [pbB], [pbB])
                        if is_last_blk:
                            D(dma(o_pool[l, 0], pb[:, :, 1:16]), [pbB], [])
                    else:
                        D(dma(o_pool[l, pc["seq"]], pb[:, :, 33:48]), [pbB], [])
                for qi in range(3):
                    pq, pqB = proj_tile(13 + qi)
                    A(act(scr[1][:, 0:tb].bitcast(BF16)[:, 0:tb], pq[:, 0:tb], AF.Square), [pqB], [scrB[1]])
                    pss, pssB = bank()
                    T(mm(pss[:, 0:tb], BLKM_B, scr[1][:, 0:tb].bitcast(BF16)[:, 0:tb]), [cb16B, scrB[1]], [pssB])
                    rsqrt_act(scr[2][:, 0:tb], pss[:, 0:tb], RMS_EPS, [pssB], [scrB[2]])
                    if qi < 2:
                        V(stt(qn[:, qi, 0:tb], pq[:, 0:tb], PV(l, 107), scr[2][:, 0:tb], ALU.mult, ALU.mult),
                          [pqB, pvB, scrB[2]], [qnB])
                    else:
                        V(stt(kf32[:, 0:tb], pq[:, 0:tb], PV(l, 108), scr[2][:, 0:tb], ALU.mult, ALU.mult),
                          [pqB, pvB, scrB[2]], [kf32B])
                for pc in B["pieces"]:
                    kb, kbB, vb, vbB = piece_bufs(pc)[4:8]
                    a0, P, nch = pc["pc0"], pc["P"], pc["nch"]
                    G(gcopy(kb[:, 128:128 + P], kf32[:, a0:a0 + P]), [kf32B], [kbB])
                    for j in range(nch):
                        pvt, pvtB = bank()
                        cj = a0 + j * 64
                        for k in range(8):
                            T(mm(pvt[0:64, 0:128], xn[:, k, cj:cj + 64], win[:, k, 2048:2176], k == 0, k == 7),
                              [xnB[k], winB[k]], [pvtB])
                        jj = (cj // 64)
                        A(act(vf32[:, jj, :], pvt[0:64, 0:128], AF.Copy), [pvtB], [vf32B])
                        V(vcopy(vb[:, 2 + j, :, :].rearrange("t h (a d) -> t h a d", a=2),
                                vf32[:, jj, :].rearrange("t (h d) -> t h d", h=2).unsqueeze(2).to_broadcast([64, 2, 2, 64])),
                          [vf32B], [vbB])
                    if pc["seq"] == 0:
                        if is_last_blk:
                            D(dma(o_k[l, 0], kf32[:, TB - 128:TB]), [kf32B], [])
                            D(dma(o_v[l, 0, 0:64, :], vf32[:, TB // 64 - 2, :]), [vf32B], [])
                            D(dma(o_v[l, 0, 64:128, :], vf32[:, TB // 64 - 1, :]), [vf32B], [])
                    else:
                        D(dma(o_k[l, pc["seq"], :, 96:128], kf32[:, a0:a0 + 32]), [kf32B], [])
                        D(dma(o_v[l, pc["seq"], 96:128, :], vf32[0:32, a0 // 64, :]), [vf32B], [])
                    for j in range(nch):
                        if pc["seq"] == 0:
                            nq = 64
                            keys = [(64 * (j + m_), j + m_, 64) for m_ in range(3) if pc["g0"] + j - 2 + m_ >= 0]
                        else:
                            nq = 32
                            keys = [(0, 0, 64), (64, 1, 64), (128, 2, 32)]
                        q0 = a0 + j * 64
                        for h in range(2):
                            ph = 64 * h
                            pS, pSB = bank()
                            uni = (nq == 64) and all(k_[2] == 64 for k_ in keys)
                            for m_, (koff, vidx, nk) in enumerate(keys):
                                T(mm(pS[0:nk, m_ * 128:m_ * 128 + 2 * nq], kb[ph:ph + 64, koff:koff + nk],
                                     qn[ph:ph + 64, :, q0:q0 + nq]), [kbB, qnB], [pSB])
                                if not uni:
                                    A(act(Et[0:nk, m_, 0:2 * nq], pS[0:nk, m_ * 128:m_ * 128 + 2 * nq], AF.Exp, scale=0.125),
                                      [pSB], [EtB])
                            if uni:
                                nkc = len(keys)
                                A(act(Et[0:64, 0:nkc, :], pS[0:64, 0:nkc * 128].rearrange("p (m c) -> p m c", m=nkc),
                                      AF.Exp, scale=0.125), [pSB], [EtB])
                            pN, pNB = bank()
                            nk_ = len(keys)
                            for m_, (koff, vidx, nk) in enumerate(keys):
                                T(mm(pN[:, 0:2 * nq], vb[0:nk, vidx, h, :], Et[0:nk, m_, 0:2 * nq], m_ == 0, m_ == nk_ - 1),
                                  [vbB, EtB], [pNB])
                            for m_, (koff, vidx, nk) in enumerate(keys):
                                T(mm(pN[:, 256:256 + 2 * nq], ONES_B[0:nk, :], Et[0:nk, m_, 0:2 * nq], m_ == 0, m_ == nk_ - 1),
                                  [cb16B, EtB], [pNB])
                            A(act(rden[:, 0:2 * nq], pN[:, 256:256 + 2 * nq], AF.Ln, bias=DVc(l, 9 + h)), [pNB, dvB], [rdenB])
                            A(act(rden[:, 0:2 * nq], rden[:, 0:2 * nq], AF.Exp, scale=-1.0), [rdenB], [rdenB])
                            for g in range(2):
                                p0, p1 = 64 * g, 64 * g + 64
                                V(tt(mix[p0:p1, 6 + h, q0:q0 + nq], pN[p0:p1, g * nq:(g + 1) * nq],
                                     rden[p0:p1, g * nq:(g + 1) * nq], ALU.mult), [pNB, rdenB], [mixB[6 + h]])
                    if pc["seq"] == 0:
                        G(gcopy(kb[:, 0:128], kb[:, TB:TB + 128]), [kbB], [kbB])
                        G(gcopy(vb[:, 0:2], vb[:, TB // 64:TB // 64 + 2]), [vbB], [vbB])

                acd_ops = S.capture
                S.capture = []
                bpool[0] = 'w'
                chs = []
                for pc in B["pieces"]:
                    Ss_, SsB_ = piece_bufs(pc)[10:12]
                    for jl in range(pc["nch"]):
                        chs.append((pc["pc0"] // 64 + jl, Ss_, SsB_))
                assert [c_[0] for c_ in chs] == [0, 1]
                hd = []
                for h4 in range(4):
                    ct, hh = h4 // 2, h4 % 2
                    ph = 64 * hh
                    hd.append((ct, hh, ph, AR3c[ct], AR3cB[ct], BKc[ct], BKcB[ct], BhKhc[ct], BhKhcB[ct],
                               cb16[ph:ph + 64, 3, ph:ph + 64]))
                P1 = [bank() for _ in range(4)]
                for step in range(6):
                    for w in range(2):
                        q0, q1, j, cj = 64 * w, 64 * w + 64, w, 64 * w
                        for h4 in range(4):
                            ct, hh, ph, A3, A3B, BK_, BK_B, BH, BHB, idb = hd[h4]
                            p1, p1B = P1[h4]
                            if step == 0:
                                T(mm(p1[q0:q1, 0:128], BK_[ph:ph + 64, j, 0, :], A3[ph:ph + 64, j, 0:2, :]), [BK_B, A3B], [p1B])
                            elif step == 1:
                                T(mm(p1[q0:q1, 128:192], BH[ph:ph + 64, 0, cj:cj + 64], idb), [BHB, cb16B], [p1B])
                            elif step == 2:
                                T(mm(p1[q0:q1, 192:256], BK_[ph:ph + 64, j, 1, :], A3[ph:ph + 64, j, 1, :]), [BK_B, A3B], [p1B])
                            elif step == 3:
                                T(mm(p1[q0:q1, 256:320], BH[ph:ph + 64, 1, cj:cj + 64], idb), [BHB, cb16B], [p1B])
                            elif step == 4:
                                T(mm(p1[q0:q1, 320:448], A3[ph:ph + 64, j, 0, :], BK_[ph:ph + 64, j, :, :]), [BK_B, A3B], [p1B])
                            else:
                                T(mm(p1[q0:q1, 448:512], A3[ph:ph + 64, j, 0, :], idb), [A3B, cb16B], [p1B])
                for h4 in range(4):
                    p1, p1B = P1[h4]
                    V(tt(Xw[:, h4, :], p1[:, 0:512], mX[:, :], ALU.mult), [p1B, cmB], [XwhB[h4]])
                V(tt(Pm[0][:, :, :], Xw[:, :, 0:64], IDUP.unsqueeze(1).to_broadcast([128, 4, 64]), ALU.add),
                  XwhB + [cfB], PmhB[0])
                Nsrc, NTsrc, NsrcB = (lambda h4: Xw[:, h4, 0:64]), (lambda h4: Xw[:, h4, 320:384]), (lambda h4: [XwhB[h4]])
                pcur = 0
                for r in range(5):
                    PN = [bank() for _ in range(4)]
                    nb_ = N2b[r % 2]
                    for half in range(2):
                        if r == 4 and half == 0:
                            continue
                        for w in range(2):
                            q0, q1 = 64 * w, 64 * w + 64
                            for h4 in range(4):
                                pn, pnB = PN[h4]
                                if half == 0:
                                    T(mm(pn[q0:q1, 0:64], NTsrc(h4)[q0:q1, :], Nsrc(h4)[q0:q1, :]), NsrcB(h4), [pnB])
                                else:
                                    T(mm(pn[q0:q1, 64:128], Nsrc(h4)[q0:q1, :], NTsrc(h4)[q0:q1, :]), NsrcB(h4), [pnB])
                    c0_ = 64 if r == 4 else 0
                    for h4 in range(4):
                        pn, pnB = PN[h4]
                        A(act(nb_[:, h4, c0_:128], pn[:, c0_:128], AF.Copy), [pnB], [N2hB[r % 2][h4]])
                    PP = [bank() for _ in range(4)]
                    for w in range(2):
                        q0, q1 = 64 * w, 64 * w + 64
                        for h4 in range(4):
                            pp, ppB = PP[h4]
                            T(mm(pp[q0:q1, 0:64], nb_[q0:q1, h4, 64:128], Pm[pcur][q0:q1, h4, :]),
                              [N2hB[r % 2][h4], PmhB[pcur][h4]], [ppB])
                    for h4 in range(4):
                        pp, ppB = PP[h4]
                        V(tt(Pm[1 - pcur][:, h4, :], Pm[pcur][:, h4, :], pp[:, 0:64], ALU.add),
                          [PmhB[pcur][h4], ppB], [PmhB[1 - pcur][h4]])
                    pcur = 1 - pcur
                    Nsrc = (lambda h4, nb_=nb_: nb_[:, h4, 0:64])
                    NTsrc = (lambda h4, nb_=nb_: nb_[:, h4, 64:128])
                    NsrcB = (lambda h4, r=r: [N2hB[r % 2][h4]])
                P3 = [bank() for _ in range(4)]
                for w in range(2):
                    q0, q1 = 64 * w, 64 * w + 64
                    for h4 in range(4):
                        p3, p3B = P3[h4]
                        T(mm(p3[q0:q1, 0:128], Pm[pcur][q0:q1, h4, :], Xw[q0:q1, h4, 384:512]), [PmhB[pcur][h4], XwhB[h4]], [p3B])
                for h4 in range(4):
                    p3, p3B = P3[h4]
                    A(act(W3[:, h4, :], p3[:, 0:128], AF.Copy), [p3B], [W3hB[h4]])
                P4 = [bank() for _ in range(4)]
                for w in range(2):
                    q0, q1, j = 64 * w, 64 * w + 64, w
                    for step in range(3):
                        for h4 in range(4):
                            ct, hh = h4 // 2, h4 % 2
                            ph = 64 * hh
                            p4, p4B = P4[h4]
                            o4 = p4[:, 128 * w:128 * w + 128]
                            if step == 0:
                                T(mm(o4, W3[q0:q1, h4, :], Xw[q0:q1, h4, 64:192], True, False), [W3hB[h4], XwhB[h4]], [p4B])
                            elif step == 1:
                                i0_ = IDN_B[0:64, :] if w == 0 else ISH_B[64:128, :]
                                T(mm(o4, i0_, Xw[q0:q1, h4, 192:320], False, False), [cb16B, XwhB[h4]], [p4B])
                            else:
                                ish = ISH_B[0:64, :] if hh == 0 else IDN_B[64:128, :]
                                T(mm(o4, ish, AR3c[ct][ph:ph + 64, j, 1:3, :], False, True, order=True), [cb16B, AR3cB[ct]], [p4B])
                for h4 in range(4):
                    p4, p4B = P4[h4]
                    src4 = p4[:, 0:256].rearrange("p (w c) -> p w c", w=2)
                    A(act(F4[:, :, h4, :], src4, AF.Copy), [p4B], [F4hB[h4]])
                for (w, Ss, SsB) in chs:
                    cj = 64 * w
                    pV, pVB = bank()
                    for ct in range(2):
                        T(tr(pV[0:64, ct * 128:(ct + 1) * 128], ubt(4 + ct)[:, cj:cj + 64], IDN), [ubtB(4 + ct), cfB], [pVB])
                    V(vcopy(vdup[:, :, :].rearrange("t h (a d) -> t h a d", a=2),
                            pV[0:64, 0:256].rearrange("t (h d) -> t h d", h=4).unsqueeze(2).to_broadcast([64, 4, 2, 64])),
                      [pVB], [vdupB])
                    A(act(S16[64:128, :, :], Ss[64:128, :, :], AF.Copy), SsB, [S16B])
                    PY = [bank() for _ in range(4)]
                    for step in range(2):
                        for h4 in range(4):
                            pY, pYB = PY[h4]
                            oY = pY[:, 0:64]
                            if step == 0:
                                T(mm(oY, S16[64:128, h4, :], F4[64:128, w, h4, 0:64], True, False), [S16B, F4hB[h4]], [pYB])
                            else:
                                T(mm(oY, vdup[:, h4, :], F4[0:64, w, h4, 0:64], False, True, order=True), [vdupB, F4hB[h4]], [pYB])
                    for h4 in range(4):
                        ct, hh = h4 // 2, h4 % 2
                        ph = 64 * hh
                        pY, pYB = PY[h4]
                        oY = pY[:, 0:64]
                        A(act(yb[ph:ph + 64, ct, cj:cj + 64], oY[ph:ph + 64, :], AF.Copy), [pYB], [ybB[ct]])
                    PS_ = [bank() for _ in range(4)]
                    for step in range(2):
                        for h4 in range(4):
                            pSb, pSbB = PS_[h4]
                            if step == 0:
                                T(mm(pSb[:, 0:128], F4[64:128, w, h4, :], S16[64:128, h4, :], True, False), [S16B, F4hB[h4]], [pSbB])
                            else:
                                T(mm(pSb[:, 0:128], F4[0:64, w, h4, :], vdup[:, h4, :], False, True, order=True), [vdupB, F4hB[h4]], [pSbB])
                    for h4 in range(4):
                        pSb, pSbB = PS_[h4]
                        A(act(Ss[64:128, h4, :], pSb[64:128, 0:128], AF.Copy), [pSbB], [SsB[h4]])
                wv_ops = S.capture
                S.capture = outer_cap
                bpool[0] = 'a'
                ia = iw = 0
                GA, GW = 3, 4
                while ia < len(acd_ops) or iw < len(wv_ops):
                    for r_ in wv_ops[iw:iw + GW]:
                        S.op(r_[0], r_[1], r_[2], r_[3], dma=r_[4], strict=r_[5])
                    iw += GW
                    for r_ in acd_ops[ia:ia + GA]:
                        S.op(r_[0], r_[1], r_[2], r_[3], dma=r_[4], strict=r_[5])
                    ia += GA
                for pc in B["pieces"]:
                    Ss, SsB = piece_bufs(pc)[10:12]
                    if pc["seq"] == 0:
                        if is_last_blk:
                            D(dma(o_rw[l, 0].rearrange("h k v -> k h v"), Ss[64:128, :, 0:64]), SsB, [])
                    else:
                        D(dma(o_rw[l, pc["seq"]].rearrange("h k v -> k h v"), Ss[64:128, :, 0:64]), SsB, [])
                for ct in range(2):
                    pmn, pmnB = bank()
                    T(mm(pmn[:, 0:tb], BLKM, yb[:, ct, 0:tb]), [cfB, ybB[ct]], [pmnB])
                    c_, c_B = scr[1][:, 0:tb], scrB[1]
                    s_, s_B = scr[2][:, 0:tb], scrB[2]
                    V(tt(c_, yb[:, ct, 0:tb], pmn[:, 0:tb], ALU.subtract), [ybB[ct], pmnB], [c_B])
                    s_h = scr[2][:, 0:tb].bitcast(BF16)[:, 0:tb]
                    A(act(s_h, c_, AF.Square), [c_B], [s_B])
                    pvr, pvrB = bank()
                    T(mm(pvr[:, 0:tb], BLKM_B, s_h), [cb16B, s_B], [pvrB])
                    rsqrt_act(s_, pvr[:, 0:tb], GN_EPS, [pvrB], [s_B])
                    V(tt(c_, c_, s_, ALU.mult), [c_B, s_B], [c_B])
                    V(ts(c_, c_, PV(l, 101 + ct), ALU.mult, PV(l, 103 + ct), ALU.add), [c_B, pvB], [c_B])
                    V(tt(c_, c_, bon[:, ct, 0:tb], ALU.add), [c_B, bonB[ct]], [c_B])
                    V(tt(mix[:, 2 + ct, 0:tb], c_, gate[:, ct, 0:tb], ALU.mult), [c_B, gateB[ct]], [mixB[2 + ct]])

                for ot in range(8):
                    po, poB = bank()
                    for k in range(8):
                        T(mm(po[:, 0:tb], wout[:, k, ot * 128:(ot + 1) * 128], mix[:, k, 0:tb], k == 0, k == 7),
                          [woutB, mixB[k]], [poB])
                    V(tt(xT[:, ot, c0:c0 + tb], xT[:, ot, c0:c0 + tb], po[:, 0:tb], ALU.add), [xB[ot][blk], poB], [xB[ot][blk]])

            def cap(fn_, *a_):
                S.capture = []
                fn_(*a_)
                ops_ = S.capture
                S.capture = None
                return ops_

            def replay(lst):
                for r_ in lst:
                    S.op(r_[0], r_[1], r_[2], r_[3], dma=r_[4], strict=r_[5])

            replay(cap(FE, blocks[0], 0))
            for i_, B in enumerate(blocks):
                be_ops = cap(BE, B, i_ % 2)
                fe_ops = cap(FE, blocks[i_ + 1], (i_ + 1) % 2) if i_ + 1 < len(blocks) else []
                nb_, nf_ = len(be_ops), len(fe_ops)
                ib_ = if_ = 0
                GB = 6
                GF = max(1, -(-GB * nf_ // max(nb_, 1)))
                while ib_ < nb_ or if_ < nf_:
                    replay(be_ops[ib_:ib_ + GB])
                    ib_ += GB
                    replay(fe_ops[if_:if_ + GF])
                    if_ += GF
                if B["blk"] == npb // 2 - 1:
                    S.epoch()
            bpool[0] = 'all'

        def ffn_phase(seg, l):
            apos[0] = 0
            S.mute = 'F' not in DBG
            FS = 256
            nsl = 2816 // FS
            wgs = [carve([8, FS], BF16) for _ in range(2)]
            wus = [carve([8, FS], BF16) for _ in range(2)]
            wds = [carve([2, 1024], BF16) for _ in range(2)]
            wsB = [[Buf(), Buf(), Buf()] for _ in range(2)]
            sqb = carve([2, TB], BF16)
            sqbB = [Buf(), Buf()]
            rstd = carve([TB], F32)
            rstdB = Buf()
            FB = 512
            sg = [carve([FB], F32) for _ in range(2)]
            sgB = [Buf(), Buf()]
            aT = carve([2, 2, FB], BF16)
            aTB = [[Buf(), Buf()], [Buf(), Buf()]]
            hn = win
            blocks = [(b, b * TB, TB) for b in range(npb)]
            if seg == 0:
                blocks.append((npb, TP, 128))

            def load_slice(j):
                bf = j % 2
                D(dma(wgs[bf][:], d_wg[l].rearrange("(k p) f -> p k f", p=128)[:, :, j * FS:(j + 1) * FS], "pool"),
                  [], [wsB[bf][0]], q="pool")
                D(dma(wus[bf][:], d_wu[l].rearrange("(k p) f -> p k f", p=128)[:, :, j * FS:(j + 1) * FS], "pool"),
                  [], [wsB[bf][1]], q="pool")
                D(dma(wds[bf][:], d_wd[l, j * FS:(j + 1) * FS, :].rearrange("(t p) o -> p t o", p=128), "pool"),
                  [], [wsB[bf][2]], q="pool")

            load_slice(0)
            for (blk, c0, tb) in blocks:
                rmsnorm_block(l, 8, blk, c0, tb, lambda k: hn[:, k, c0:c0 + tb], lambda k: winB[k], sqb, sqbB, rstd, rstdB)
            fblocks = []
            g = FB // TB
            for b0 in range(0, npb, g):
                ids = list(range(b0, min(npb, b0 + g)))
                fblocks.append((ids, b0 * TB, len(ids) * TB))
            if seg == 0:
                fblocks.append(([npb], TP, 128))
            it = 0
            for j in range(nsl):
                if j + 1 < nsl:
                    load_slice(j + 1)
                bf = j % 2
                for (ids, c0, tb) in fblocks:
                    ab = it % 2
                    it += 1
                    for dt_ in range(2):
                        pg, pgB = bank()
                        for k in range(8):
                            T(mm(pg[:, 0:tb], wgs[bf][:, k, dt_ * 128:(dt_ + 1) * 128], hn[:, k, c0:c0 + tb], k == 0, k == 7),
                              [wsB[bf][0], winB[k]], [pgB])
                        pu, puB = bank()
                        for k in range(8):
                            T(mm(pu[:, 0:tb], wus[bf][:, k, dt_ * 128:(dt_ + 1) * 128], hn[:, k, c0:c0 + tb], k == 0, k == 7),
                              [wsB[bf][1], winB[k]], [puB])
                        A(act(sg[dt_][:, 0:tb], pg[:, 0:tb], AF.Silu), [pgB], [sgB[dt_]])
                        V(tt(aT[:, ab, dt_, 0:tb], sg[dt_][:, 0:tb], pu[:, 0:tb], ALU.mult), [sgB[dt_], puB], [aTB[ab][dt_]])
                    for ot in range(8):
                        po, poB = bank()
                        for dt_ in range(2):
                            T(mm(po[:, 0:tb], wds[bf][:, dt_, ot * 128:(ot + 1) * 128], aT[:, ab, dt_, 0:tb], dt_ == 0, dt_ == 1),
                              [wsB[bf][2], aTB[ab][dt_]], [poB])
                        xb_ = [xB[ot][i_] for i_ in ids]
                        V(tt(xT[:, ot, c0:c0 + tb], xT[:, ot, c0:c0 + tb], po[:, 0:tb], ALU.add), xb_ + [poB], xb_)
                        if j == nsl - 1 and l == NL - 1:
                            if ids[0] == npb:
                                D(dma(o_ys[ot * 128:(ot + 1) * 128, :], xT[:, ot, TP:TP + 128]), xb_, [])
                            else:
                                D(dma(o_yp[seg, ot * 128:(ot + 1) * 128, c0:c0 + tb], xT[:, ot, c0:c0 + tb]), xb_, [])

        for seg in range(n_seg):
            load_segment(seg)
            for l in range(NL):
                S.epoch()
                mixer_phase(seg, l)
                S.epoch()
                ffn_phase(seg, l)
            S.mute = False
            store_segment(seg)
            S.barrier()
        S.emit()
        build.stats = S.stats
    return nc


def _consts():
    cf = np.zeros((128, 8, 128), np.float32)
    cf[:, 0, :] = np.eye(128, dtype=np.float32)
    cf[:, 1, :] = 1.0 / 1024.0
    cf[:, 2, :] = 1.0 / 256.0
    blk = np.zeros((128, 128), np.float32)
    blk[0:64, 0:64] = 1.0
    blk[64:128, 64:128] = 1.0
    cf[:, 3, :] = blk / 64.0
    cf[:, 4, :] = blk
    cf[:, 5, :] = 1.0
    cf[0:64, 6, 64:128] = np.eye(64, dtype=np.float32)
    for p in range(128):
        cf[p, 7, p % 64] = 1.0
    i = np.arange(64)[:, None]
    t = np.arange(64)[None, :]
    m1 = np.ones((64, 512), np.float32)
    m1[:, 0:64] = (i < t)
    m1[:, 64:128] = (i <= t)
    m1[:, 192:256] = (i <= t)
    m1[:, 320:384] = (i > t)
    m1[:, 384:448] = (i > t)
    m2 = np.zeros((64, 128), np.float32)
    m2[:, 0:64] = (i > t)
    m2[:, 64:128] = (i > t)
    scanm = np.ones((128, TB), np.float32)
    scanm[:, ::64] = 0.0
    icnt = np.zeros((128, 2, TB), np.float32)
    wins = {(0, 0): 2, (0, 1): 4, (1, 0): 8, (1, 1): 16}
    pos = np.arange(TB)
    for ct in range(2):
        for hf in range(2):
            w = wins[(ct, hf)]
            icnt[hf * 64:(hf + 1) * 64, ct, :] = 1.0 / np.minimum(w, pos + 1).astype(np.float32)[None, :]
    m1 = np.concatenate([m1, m1], axis=0)
    cf[64:128, 6, 0:64] = np.eye(64, dtype=np.float32)
    return cf, m1, m2, scanm, icnt


def _col(v):
    v = np.asarray(v, np.float32)
    return v.reshape(-1, 128).T


def prepare_shared(inp):
    f = lambda k: np.asarray(inp[k], np.float32)
    pvs = np.zeros((2, 128, NPV), np.float32)
    for l in range(2):
        p = pvs[l]
        p[:, 0:8] = _col(f("norm_mix_g")[l])
        p[:, 8:16] = _col(f("norm_ffn_g")[l])
        cw = f("conv_w")[l]
        for ct in range(2):
            p[:, 16 + ct * 31:16 + (ct + 1) * 31] = cw[:, ct * 128:(ct + 1) * 128].T
        p[:, 78:80] = _col(f("conv_b")[l])
        p[:, 80:82] = _col(f("conv_ln_g")[l])
        p[:, 82:84] = _col(f("conv_ln_b")[l])
        p[:, 84:91] = _col(f("rwkv_mu")[l])
        p[:, 91:93] = _col(f("rwkv_w0")[l])
        p[:, 93:95] = _col(f("rwkv_a0")[l])
        p[:, 95:97] = _col(f("rwkv_k_k")[l])
        p[:, 97:99] = _col(f("rwkv_k_a")[l])
        p[:, 99:101] = _col(f("rwkv_r_k")[l].reshape(-1))
        p[:, 101:103] = _col(f("rwkv_gn_g")[l])
        p[:, 103:105] = _col(f("rwkv_gn_b")[l])
        p[:, 105:107] = _col(f("pool_scale")[l])
        p[:, 107] = np.tile(f("attn_q_norm")[l], 2)
        p[:, 108] = np.tile(f("attn_k_norm")[l], 2)
        sk = f("attn_sinks")[l]
        for h in range(2):
            p[0:64, 109 + h] = sk[2 * h]
            p[64:128, 109 + h] = sk[2 * h + 1]
    w_in = f("w_in")
    base = 512 + 896 + 256
    qcols = lambda h: list(range(base + 64 * h, base + 64 * (h + 1)))
    perm = list(range(base)) + qcols(0) + qcols(2) + qcols(1) + qcols(3) + list(range(base + 256, 2176))
    w_in_p = np.ascontiguousarray(w_in[:, :, perm])
    lw = np.concatenate([f("rwkv_w2"), f("rwkv_a2"), f("rwkv_g2")], axis=1)
    pw = f("pool_w")
    poolw = np.zeros((2, 2, 128, 128), np.float32)
    for l in range(2):
        for g in range(4):
            ct, hf = g // 2, g % 2
            poolw[l, ct, hf * 64:(hf + 1) * 64, hf * 64:(hf + 1) * 64] = pw[l, g]
    cf, m1, m2, scanm, icnt = _consts()
    return dict(pv=pvs, w_in=w_in_p, w_out=f("w_out"), wg=f("ffn_w_gate"), wu=f("ffn_w_up"), wd=f("ffn_w_down"),
                lw=np.ascontiguousarray(lw), poolw=poolw, cf=cf, m1=m1, m2=m2, scanm=scanm, icnt=icnt)


def prepare_core(inp, c, n_seg, npb):
    f = lambda k: np.asarray(inp[k], np.float32)
    TP = npb * TB
    b = c % 4
    xp = f("x_prompt")[b, :n_seg * TP]
    xTp = np.ascontiguousarray(xp.reshape(n_seg, TP, 1024).transpose(0, 2, 1))
    xs = f("x_sample")[2 * c:2 * c + 2]
    xTs = np.zeros((1024, 128), np.float32)
    for s in range(2):
        xTs[:, 64 * s:64 * s + 32] = xs[s].T
    sl = slice(2 * c, 2 * c + 2)
    cconv = f("cache_conv")[:, sl]
    cconv = np.ascontiguousarray(cconv.reshape(2, 2, 30, 2, 128).transpose(0, 1, 4, 3, 2))
    srw = np.ascontiguousarray(f("state_rwkv")[:, sl].transpose(0, 1, 2, 4, 3))
    ssh = f("state_rwkv_shift")[:, sl]
    ssh = np.ascontiguousarray(ssh.reshape(2, 2, 7, 128).transpose(0, 1, 3, 2))
    cpool = f("cache_pool")[:, sl]
    cpool = np.ascontiguousarray(cpool.reshape(2, 2, 15, 2, 128).transpose(0, 1, 4, 3, 2))
    ck = f("cache_k")[:, sl].reshape(2, 2, 128, 128)
    ckT = np.ascontiguousarray(ck.transpose(0, 1, 3, 2))
    cv = np.ascontiguousarray(f("cache_v")[:, sl].reshape(2, 2, 128, 128))
    return dict(xTp=xTp, xTs=xTs, cconv=cconv, srw=srw, sshift=ssh, cpool=cpool, ckT=ckT, cv=cv)


_NC_CACHE = {}


def run(inp, n_seg, npb, n_cores=8):
    key = (n_seg, npb)
    if key not in _NC_CACHE:
        _NC_CACHE[key] = build(n_seg, npb)
    nc = _NC_CACHE[key]
    shared = prepare_shared(inp)
    in_maps = []
    for c in range(n_cores):
        m = dict(shared)
        m.update(prepare_core(inp, c, n_seg, npb))
        in_maps.append(m)
    res = run_bass_kernel_spmd(nc, in_maps, core_ids=list(range(n_cores)))
    R = res.results
    TP = npb * TB
    T = n_seg * TP
    nb = min(4, n_cores)
    y_p = np.zeros((4, T, 1024), np.float32)
    for b in range(nb):
        y_p[b] = R[b]["yTp"].transpose(0, 2, 1).reshape(T, 1024)
    ns = 2 * n_cores
    y_s = np.zeros((16, 32, 1024), np.float32)
    conv_p = np.zeros((2, 4, 30, 256), np.float32)
    conv_s = np.zeros((2, 16, 30, 256), np.float32)
    rw_p = np.zeros((2, 4, 4, 64, 64), np.float32)
    rw_s = np.zeros((2, 16, 4, 64, 64), np.float32)
    sh_p = np.zeros((2, 4, 896), np.float32)
    sh_s = np.zeros((2, 16, 896), np.float32)
    pool_p = np.zeros((2, 4, 15, 256), np.float32)
    pool_s = np.zeros((2, 16, 15, 256), np.float32)
    k_p = np.zeros((2, 4, 128, 2, 64), np.float32)
    k_s = np.zeros((2, 16, 128, 2, 64), np.float32)
    v_p = np.zeros((2, 4, 128, 2, 64), np.float32)
    v_s = np.zeros((2, 16, 128, 2, 64), np.float32)

    def put(dst_p, dst_s, name, conv):
        for c in range(n_cores):
            o = R[c][name]
            if c < nb:
                dst_p[:, c] = conv(o[:, 0])
            for s in range(2):
                dst_s[:, 2 * c + s] = conv(o[:, 1 + s])

    for c in range(n_cores):
        ys = R[c]["yTs"]
        for s in range(2):
            y_s[2 * c + s] = ys[:, 64 * s:64 * s + 32].T
    put(conv_p, conv_s, "o_conv", lambda o: o.transpose(0, 3, 2, 1).reshape(2, 30, 256))
    put(rw_p, rw_s, "o_rw", lambda o: o.transpose(0, 1, 3, 2))
    put(sh_p, sh_s, "o_shift", lambda o: o.transpose(0, 2, 1).reshape(2, 896))
    put(pool_p, pool_s, "o_pool", lambda o: o.transpose(0, 3, 2, 1).reshape(2, 15, 256))
    put(k_p, k_s, "o_k", lambda o: o.transpose(0, 2, 1).reshape(2, 128, 2, 64))
    put(v_p, v_s, "o_v", lambda o: o.reshape(2, 128, 2, 64))
    return (y_p, y_s, conv_p, conv_s, rw_p, rw_s, sh_p, sh_s, pool_p, pool_s, k_p, k_s, v_p, v_s)


def kernel(**inputs):
    return run(inputs, 2, 2048 // TB, 8)
```
